# Optimizing a Trainium2 kernel written in Bass

```python
import jax, jax.numpy as jnp
from jax import lax
import numpy as np

D_MODEL = 1024
BATCH = 8
SEQ = 4096
DEPTH = 1
DEC_BATCH = 8
DEC_SEQ = 64
PAST_LEN = 1024

CHUNK = 64
Q_BLOCK = 128
CONV_WIDTH = 3
D_CONV = D_MODEL
N_HEADS = 16
QK_NOPE = 64
QK_ROPE = 32
V_HEAD = 64
Q_LORA = 512
KV_LORA = 256
ROPE_THETA = 10000.0
N_MEM = 256
MEM_HEADS = 4
MEM_HEAD_DIM = D_MODEL // MEM_HEADS
D_FF = 4 * D_MODEL
EPS = 1e-6
IN_COLS = 3 * D_CONV + Q_LORA + KV_LORA + QK_ROPE + 2 * D_MODEL
ATTN_SCALE = (QK_NOPE + QK_ROPE) ** -0.5
MEM_SCALE = MEM_HEAD_DIM ** -0.5

kernel_name = 'hybrid_shortconv_mla_stream_step'


def rms_norm(x, g):
    x32 = x.astype(jnp.float32)
    y = x32 * lax.rsqrt(jnp.mean(x32 * x32, axis=-1, keepdims=True) + EPS)
    return (y * g.astype(jnp.float32)).astype(x.dtype)


def rope(x, pos):
    half = QK_ROPE // 2
    inv = ROPE_THETA ** (-jnp.arange(half, dtype=jnp.float32) / half)
    ang = pos.astype(jnp.float32)[:, None] * inv[None, :]
    ang = ang.reshape((ang.shape[0],) + (1,) * (x.ndim - 3) + (half,))
    cos, sin = jnp.cos(ang), jnp.sin(ang)
    x32 = x.astype(jnp.float32)
    x1, x2 = x32[..., :half], x32[..., half:]
    return jnp.concatenate([x1 * cos - x2 * sin, x2 * cos + x1 * sin], axis=-1).astype(x.dtype)


def split_in(z):
    sizes = (D_CONV, D_CONV, D_CONV, Q_LORA, KV_LORA, QK_ROPE, D_MODEL, D_MODEL)
    idx = np.cumsum(sizes)[:-1].tolist()
    return jnp.split(z, idx, axis=-1)


def short_conv(v, buf, w_conv):
    T = v.shape[1]
    vp = jnp.concatenate([buf.astype(v.dtype), v], axis=1)
    y = vp[:, 0:T] * w_conv[0]
    for k in range(1, CONV_WIDTH):
        y = y + vp[:, k:k + T] * w_conv[k]
    return y, vp[:, T:]


def mla_keys_values(ckv, kr, w_ukv):
    Bn, L = ckv.shape[:2]
    kv = (ckv @ w_ukv).reshape(Bn, L, N_HEADS, QK_NOPE + V_HEAD)
    k = jnp.concatenate([kv[..., :QK_NOPE], jnp.broadcast_to(kr[:, :, None, :], (Bn, L, N_HEADS, QK_ROPE))], axis=-1)
    return k, kv[..., QK_NOPE:]


def chunk_causal_attend(q, k, v, q_pos, k_pos):
    s = jnp.einsum('bqhd,bkhd->bhqk', q, k).astype(jnp.float32) * ATTN_SCALE
    visible = (k_pos[None, :] // CHUNK) <= (q_pos[:, None] // CHUNK)
    p = jax.nn.softmax(jnp.where(visible, s, -jnp.inf), axis=-1).astype(v.dtype)
    return jnp.einsum('bhqk,bkhd->bqhd', p, v)


def blocked_attend(q, k, v, k_pos):
    Bn, T = q.shape[:2]
    nblk = T // Q_BLOCK
    qb = q.reshape(Bn, nblk, Q_BLOCK, N_HEADS, q.shape[-1]).transpose(1, 0, 2, 3, 4)
    starts = jnp.arange(nblk, dtype=jnp.int32) * Q_BLOCK

    def one(args):
        q_blk, start = args
        return chunk_causal_attend(q_blk, k, v, start + jnp.arange(Q_BLOCK, dtype=jnp.int32), k_pos)

    o = lax.map(one, (qb, starts))
    return o.transpose(1, 0, 2, 3, 4).reshape(Bn, T, N_HEADS, V_HEAD)


def memory_kv(mem, g, w_k, w_v):
    Bn, M = mem.shape[:2]
    mn = rms_norm(mem, g)
    k = (mn @ w_k).reshape(Bn, M, MEM_HEADS, MEM_HEAD_DIM)
    v = (mn @ w_v).reshape(Bn, M, MEM_HEADS, MEM_HEAD_DIM)
    return k, v


def memory_attend(h, mem_k, mem_v, w_q, w_o):
    Bn, T = h.shape[:2]
    q = (h @ w_q).reshape(Bn, T, MEM_HEADS, MEM_HEAD_DIM)
    s = jnp.einsum('bqhd,bkhd->bhqk', q, mem_k).astype(jnp.float32) * MEM_SCALE
    p = jax.nn.softmax(s, axis=-1).astype(mem_v.dtype)
    o = jnp.einsum('bhqk,bkhd->bqhd', p, mem_v).reshape(Bn, T, MEM_HEADS * MEM_HEAD_DIM)
    return o @ w_o


def layer(x, conv_buf, past_ckv, past_kr, mem_k, mem_v, blocked,
          g_mix, w_in, w_conv, w_conv_out, g_q, w_uq, g_kv, w_ukv, w_mla_out, w_mix_out,
          g_mem_q, w_qm, w_om, g_mlp, w_up, w_down):
    Bn, T = x.shape[:2]
    past = past_ckv.shape[1]
    pos = past + jnp.arange(T, dtype=jnp.int32)
    k_pos = jnp.arange(past + T, dtype=jnp.int32)
    n = rms_norm(x, g_mix)
    u, gate_b, gate_c, cq, ckv_raw, kr_raw, a_conv, a_mla = split_in(n @ w_in)
    y_sc, new_buf = short_conv(gate_c * u, conv_buf, w_conv)
    y_a = (gate_b * y_sc) @ w_conv_out
    q = (rms_norm(cq, g_q) @ w_uq).reshape(Bn, T, N_HEADS, QK_NOPE + QK_ROPE)
    q = jnp.concatenate([q[..., :QK_NOPE], rope(q[..., QK_NOPE:], pos)], axis=-1)
    ckv = rms_norm(ckv_raw, g_kv)
    kr = rope(kr_raw, pos)
    k, v = mla_keys_values(jnp.concatenate([past_ckv.astype(ckv.dtype), ckv], axis=1),
                           jnp.concatenate([past_kr.astype(kr.dtype), kr], axis=1), w_ukv)
    if blocked:
        o = blocked_attend(q, k, v, k_pos)
    else:
        o = chunk_causal_attend(q, k, v, pos, k_pos)
    y_b = o.reshape(Bn, T, N_HEADS * V_HEAD) @ w_mla_out
    x = x + (jax.nn.sigmoid(a_conv) * y_a + jax.nn.sigmoid(a_mla) * y_b) @ w_mix_out
    x = x + memory_attend(rms_norm(x, g_mem_q), mem_k, mem_v, w_qm, w_om)
    hm = rms_norm(x, g_mlp)
    x = x + jnp.square(jax.nn.relu(hm @ w_up)) @ w_down
    return x, new_buf, ckv, kr


def setup_inputs(seed: int = 0) -> dict:
    key = jax.random.key(seed)
    ks = jax.random.split(key, 32)

    def nrm(k, shape, scale):
        return jax.random.normal(k, shape, jnp.float32) * scale

    def gain(k, shape):
        return 1.0 + 0.01 * jax.random.normal(k, shape, jnp.float32)

    return {
        'x_prompt': nrm(ks[0], (BATCH, SEQ, D_MODEL), 1.0),
        'x_sample': nrm(ks[1], (DEC_BATCH, DEC_SEQ, D_MODEL), 1.0),
        'cache_conv': nrm(ks[2], (DEPTH, DEC_BATCH, CONV_WIDTH - 1, D_CONV), 1.0),
        'cache_ckv': nrm(ks[3], (DEPTH, DEC_BATCH, PAST_LEN, KV_LORA), 1.0),
        'cache_krope': nrm(ks[4], (DEPTH, DEC_BATCH, PAST_LEN, QK_ROPE), 1.0),
        'cache_mem_k': nrm(ks[5], (DEPTH, DEC_BATCH, N_MEM, MEM_HEADS, MEM_HEAD_DIM), 1.0),
        'cache_mem_v': nrm(ks[6], (DEPTH, DEC_BATCH, N_MEM, MEM_HEADS, MEM_HEAD_DIM), 1.0),
        'mem_prompt': nrm(ks[7], (BATCH, N_MEM, D_MODEL), 1.0),
        'g_mix': gain(ks[8], (DEPTH, D_MODEL)),
        'w_in': nrm(ks[9], (DEPTH, D_MODEL, IN_COLS), D_MODEL ** -0.5),
        'w_conv': nrm(ks[10], (DEPTH, CONV_WIDTH, D_CONV), CONV_WIDTH ** -0.5),
        'w_conv_out': nrm(ks[11], (DEPTH, D_CONV, D_MODEL), D_CONV ** -0.5),
        'g_q': gain(ks[12], (DEPTH, Q_LORA)),
        'w_uq': nrm(ks[13], (DEPTH, Q_LORA, N_HEADS * (QK_NOPE + QK_ROPE)), Q_LORA ** -0.5),
        'g_kv': gain(ks[14], (DEPTH, KV_LORA)),
        'w_ukv': nrm(ks[15], (DEPTH, KV_LORA, N_HEADS * (QK_NOPE + V_HEAD)), KV_LORA ** -0.5),
        'w_mla_out': nrm(ks[16], (DEPTH, N_HEADS * V_HEAD, D_MODEL), (N_HEADS * V_HEAD) ** -0.5),
        'w_mix_out': nrm(ks[17], (DEPTH, D_MODEL, D_MODEL), D_MODEL ** -0.5),
        'g_mem_q': gain(ks[18], (DEPTH, D_MODEL)),
        'g_mem_kv': gain(ks[19], (DEPTH, D_MODEL)),
        'w_qm': nrm(ks[20], (DEPTH, D_MODEL, MEM_HEADS * MEM_HEAD_DIM), D_MODEL ** -0.5),
        'w_km': nrm(ks[21], (DEPTH, D_MODEL, MEM_HEADS * MEM_HEAD_DIM), D_MODEL ** -0.5),
        'w_vm': nrm(ks[22], (DEPTH, D_MODEL, MEM_HEADS * MEM_HEAD_DIM), D_MODEL ** -0.5),
        'w_om': nrm(ks[23], (DEPTH, MEM_HEADS * MEM_HEAD_DIM, D_MODEL), D_MODEL ** -0.5),
        'g_mlp': gain(ks[24], (DEPTH, D_MODEL)),
        'w_up': nrm(ks[25], (DEPTH, D_MODEL, D_FF), D_MODEL ** -0.5),
        'w_down': nrm(ks[26], (DEPTH, D_FF, D_MODEL), D_FF ** -0.5),
        'g_final': gain(ks[27], (D_MODEL,)),
    }


def reference(x_prompt, x_sample, cache_conv, cache_ckv, cache_krope, cache_mem_k, cache_mem_v, mem_prompt,
              g_mix, w_in, w_conv, w_conv_out, g_q, w_uq, g_kv, w_ukv, w_mla_out, w_mix_out,
              g_mem_q, g_mem_kv, w_qm, w_km, w_vm, w_om, g_mlp, w_up, w_down, g_final):
    xp, xs = x_prompt, x_sample
    Bp = xp.shape[0]
    p_conv, p_ckv, p_kr, p_mk, p_mv = [], [], [], [], []
    s_conv, s_ckv, s_kr = [], [], []
    for l in range(DEPTH):
        shared = (g_mix[l], w_in[l], w_conv[l], w_conv_out[l], g_q[l], w_uq[l], g_kv[l], w_ukv[l],
                  w_mla_out[l], w_mix_out[l], g_mem_q[l], w_qm[l], w_om[l], g_mlp[l], w_up[l], w_down[l])
        mk, mv = memory_kv(mem_prompt, g_mem_kv[l], w_km[l], w_vm[l])
        xp, b_p, c_p, r_p = layer(xp,
                                  jnp.zeros((Bp, CONV_WIDTH - 1, D_CONV), xp.dtype),
                                  jnp.zeros((Bp, 0, KV_LORA), xp.dtype),
                                  jnp.zeros((Bp, 0, QK_ROPE), xp.dtype),
                                  mk, mv, True, *shared)
        xs, b_s, c_s, r_s = layer(xs, cache_conv[l], cache_ckv[l], cache_krope[l],
                                  cache_mem_k[l], cache_mem_v[l], False, *shared)
        p_conv.append(b_p); p_ckv.append(c_p); p_kr.append(r_p); p_mk.append(mk); p_mv.append(mv)
        s_conv.append(b_s); s_ckv.append(c_s); s_kr.append(r_s)
    y_prompt = rms_norm(xp, g_final)
    y_sample = rms_norm(xs, g_final)
    return (y_prompt, y_sample,
            jnp.stack(p_conv), jnp.stack(p_ckv), jnp.stack(p_kr), jnp.stack(p_mk), jnp.stack(p_mv),
            jnp.stack(s_conv), jnp.stack(s_ckv), jnp.stack(s_kr))
```

```python
import numpy as np
from contextlib import ExitStack
import concourse.bass as bass
import concourse.mybir as mybir
from concourse.bass_utils import run_bass_kernel_spmd

F32 = mybir.dt.float32
BF16 = mybir.dt.bfloat16
ALU = mybir.AluOpType
AF = mybir.ActivationFunctionType

ENGS = ['pe', 'act', 'dve', 'pool', 'sp']

D = 1024
T = 4096
TS = 64
PAST = 1024
NH = 16
DFF = 4096
NMEM = 256
EPS = 1e-6
ATTN_SCALE = 96.0 ** -0.5
MEM_SCALE = 256.0 ** -0.5
IN_COLS = 5920
N_CORES = 8
DMA_RING = 8


class _Op:
    __slots__ = ('eng', 'fn', 'deps', 'is_dma', 'signal', 'sem', 'val', 'dma_j', 'carry', 'odd')


class Prog:
    def __init__(self, nc, es, dma_ring=8):
        self.nc = nc
        self.ring = dma_ring
        self.es = es
        self.nodd = 0
        self.esem = {e: es.enter_context(nc.semaphore('s_' + e)) for e in ENGS}
        self.dsem = {e: [es.enter_context(nc.semaphore('d_%s_%d' % (e, i))) for i in range(dma_ring)]
                     for e in ('sp', 'pool', 'act')}
        self.rank = {e: 0 for e in ENGS}
        self.ndma = {e: 0 for e in ENGS}
        self.dma_hist = {e: [] for e in ENGS}
        self.carry = []
        self._reset()

    def _reset(self):
        self.ops = {e: [] for e in ENGS}
        self.cells = {}

    def emit(self, eng, fn, reads=(), writes=(), dma=False, carry=False, odd=False):
        op = _Op()
        op.odd = odd
        op.eng = eng
        op.fn = fn
        op.is_dma = dma
        op.signal = dma
        op.sem = None
        op.val = 0
        op.dma_j = -1
        op.carry = carry
        deps = {}
        for c in reads:
            st = self.cells.get(c)
            if st is not None and st[0] is not None:
                deps[id(st[0])] = (st[0], True)
            if st is not None and eng != 'pe' and isinstance(c, str) and c.startswith('ps'):
                for r in st[1]:
                    if r.eng != eng and id(r) not in deps:
                        deps[id(r)] = (r, False)
        for c in writes:
            st = self.cells.get(c)
            if st is None:
                continue
            if st[0] is not None and id(st[0]) not in deps:
                deps[id(st[0])] = (st[0], False)
            for r in st[1]:
                if id(r) not in deps:
                    deps[id(r)] = (r, False)
        real = []
        for d, raw in deps.values():
            if d.eng == eng and not d.is_dma and not dma and eng == 'pe':
                continue
            real.append(d)
            d.signal = True
        op.deps = real
        for c in reads:
            st = self.cells.setdefault(c, [None, []])
            if not dma:
                st[1] = [r for r in st[1] if r.is_dma or r.eng != eng]
            st[1].append(op)
        for c in writes:
            self.cells[c] = [op, []]
        self.ops[eng].append(op)
        return op

    def run(self, wait_carry_on=None, final=False):
        nc = self.nc
        last_tok = {}
        for e in ENGS:
            comp = [op for op in self.ops[e] if not op.is_dma]
            if comp:
                comp[-1].signal = True
        for e in ENGS:
            for op in self.ops[e]:
                if op.is_dma and op.odd:
                    op.sem = self.es.enter_context(nc.semaphore('o_%d' % self.nodd))
                    self.nodd += 1
                    op.val = 16
                elif op.is_dma:
                    op.dma_j = self.ndma[e]
                    self.ndma[e] += 1
                    op.sem = self.dsem[e][op.dma_j % self.ring]
                    op.val = 16 * (op.dma_j // self.ring + 1)
                    self.dma_hist[e].append((op.sem, op.val))
                elif op.signal:
                    self.rank[e] += 1
                    op.sem = self.esem[e]
                    op.val = self.rank[e]
            comp = [op for op in self.ops[e] if not op.is_dma]
            if comp:
                last_tok[e] = (comp[-1].sem, comp[-1].val)
        bar_dma = {}
        new_carry = []
        for e in ENGS:
            for op in self.ops[e]:
                if op.is_dma:
                    if op.carry:
                        new_carry.append((op.sem, op.val))
                    else:
                        k = id(op.sem)
                        if k not in bar_dma or bar_dma[k][1] < op.val:
                            bar_dma[k] = (op.sem, op.val)
        old_carry = list(self.carry)
        if wait_carry_on is not None:
            self.carry = []
        self.carry += new_carry
        if final:
            for sem, val in self.carry:
                k = id(sem)
                if k not in bar_dma or bar_dma[k][1] < val:
                    bar_dma[k] = (sem, val)
        ring = self.ring
        with nc.Block() as block:
            hooks = {'pe': block.tensor, 'act': block.scalar, 'dve': block.vector,
                     'pool': block.gpsimd, 'sp': block.sync}

            def make(e):
                def body(eng):
                    waited = {}

                    def wait(sem, val):
                        k = id(sem)
                        if waited.get(k, 0) >= val:
                            return
                        waited[k] = val
                        eng.wait_ge(sem, val)

                    if wait_carry_on == e:
                        for sem, val in old_carry:
                            wait(sem, val)
                    for op in self.ops[e]:
                        if op.is_dma and not op.odd and op.dma_j >= ring:
                            ps, pv = self.dma_hist[e][op.dma_j - ring]
                            wait(ps, pv)
                        for d in op.deps:
                            wait(d.sem, d.val)
                        ins = op.fn(eng)
                        if op.is_dma:
                            ins.then_inc(op.sem, 16)
                        elif op.signal:
                            ins.then_inc(op.sem, 1)
                    for f in ENGS:
                        if f != e and f in last_tok:
                            wait(*last_tok[f])
                    for sem, val in bar_dma.values():
                        wait(sem, val)
                return body

            for e in ENGS:
                hooks[e](make(e))
        self._reset()


class RR:
    def __init__(self, items):
        self.items = items
        self.i = 0

    def get(self):
        it = self.items[self.i % len(self.items)]
        self.i += 1
        return it


def build_program(phases='ABC', conv=True, dbg_stage=9, dbg_tiles=99):
    nc = bass.Bass("TRN2", target_bir_lowering=False)

    def din(name, shape):
        return nc.dram_tensor(name, list(shape), F32, kind="ExternalInput").ap()

    def dout(name, shape):
        return nc.dram_tensor(name, list(shape), F32, kind="ExternalOutput").ap()

    x_p = din("x_p", [T, D]); x_s = din("x_s", [TS, D])
    cconv = din("cconv", [2, D]); cckv = din("cckv", [PAST, 256]); ckr = din("ckr", [PAST, 32])
    cmk = din("cmk", [NMEM, D]); cmv = din("cmv", [NMEM, D]); memp = din("memp", [NMEM, D])
    g_mix = din("g_mix", [D]); w_in = din("w_in", [D, IN_COLS]); w_conv = din("w_conv", [3, D])
    w_conv_out = din("w_conv_out", [D, D]); g_q = din("g_q", [512]); w_uqn = din("w_uqn", [512, 1024])
    w_uqr = din("w_uqr", [512, 512])
    w_uqs = din("w_uqs", [512, 512]); g_kv = din("g_kv", [256]); w_ukv = din("w_ukv", [256, 2048])
    w_mla_out = din("w_mla_out", [D, D]); w_mix_out = din("w_mix_out", [D, D])
    g_mem_q = din("g_mem_q", [D]); g_mem_kv = din("g_mem_kv", [D])
    w_qm = din("w_qm", [D, D]); w_km = din("w_km", [D, D]); w_vm = din("w_vm", [D, D]); w_om = din("w_om", [D, D])
    g_mlp = din("g_mlp", [D]); w_up = din("w_up", [D, DFF]); w_down = din("w_down", [DFF, D]); g_final = din("g_final", [D])
    identf = din("identf", [128, 128])
    gcols_h = din("gcols_h", [128, 32])
    wconv_h = din("wconv_h", [128, 24])
    cconv_h = din("cconv_h", [128, 16])
    rt_tm = din("rt_tm", [T + TS, 64]); rt_c = din("rt_c", [128, T + TS]); rt_s = din("rt_s", [128, T + TS])

    y_p = dout("y_p", [T, D]); y_s = dout("y_s", [TS, D])
    nconv_p = dout("nconv_p", [128, 16]); nckv_p = dout("nckv_p", [T, 256]); nkr_p = dout("nkr_p", [T, 32])
    nmk_p = dout("nmk_p", [NMEM, D]); nmv_p = dout("nmv_p", [NMEM, D])
    nconv_s = dout("nconv_s", [128, 16]); nckv_s = dout("nckv_s", [TS, 256]); nkr_s = dout("nkr_s", [TS, 32])

    pan_spec = {}
    def addpan(name, src, r0, c0):
        pan_spec[name] = (len(pan_spec), src, r0, c0)
    for i in range(2):
        addpan('u%d' % i, w_in, 0, 0 + 512 * i)
        addpan('gb%d' % i, w_in, 0, 1024 + 512 * i)
        addpan('gc%d' % i, w_in, 0, 2048 + 512 * i)
        addpan('ac%d' % i, w_in, 0, 3872 + 512 * i)
        addpan('am%d' % i, w_in, 0, 4896 + 512 * i)
        addpan('co%d' % i, w_conv_out, 0, 512 * i)
        addpan('mo%d' % i, w_mla_out, 0, 512 * i)
        addpan('mx%d' % i, w_mix_out, 0, 512 * i)
        addpan('qm%d' % i, w_qm, 0, 512 * i)
        addpan('om%d' % i, w_om, 0, 512 * i)
        addpan('km%d' % i, w_km, 0, 512 * i)
        addpan('vm%d' % i, w_vm, 0, 512 * i)
    for j in range(8):
        addpan('up%d' % j, w_up, 0, 512 * j)
    for g in range(4):
        for h in range(2):
            addpan('dn%d_%d' % (g, h), w_down, 1024 * g, 512 * h)
    NPAN = len(pan_spec)
    wsc = nc.dram_tensor("wsc", [NPAN, 128, 4096], BF16, kind="Internal").ap()
    qscn = nc.dram_tensor("qscn", [8, 128, T], BF16, kind="Internal").ap()
    qscr = nc.dram_tensor("qscr", [4, 128, T], BF16, kind="Internal").ap()
    qscn_s = nc.dram_tensor("qscn_s", [8, 128, TS], BF16, kind="Internal").ap()
    qscr_s = nc.dram_tensor("qscr_s", [4, 128, TS], BF16, kind="Internal").ap()
    osc = nc.dram_tensor("osc", [8, 128, T], BF16, kind="Internal").ap()
    osc_s = nc.dram_tensor("osc_s", [8, 128, TS], BF16, kind="Internal").ap()

    with ExitStack() as gs:
        P = Prog(nc, gs, dma_ring=DMA_RING)

        def dma(q, out, in_, reads, writes, carry=False, odd=False):
            P.emit(q, lambda e: e.dma_start(out=out, in_=in_), reads, writes, dma=True, carry=carry, odd=odd)

        def mm(out, lhsT, rhs, start, stop, reads, writes):
            P.emit('pe', lambda e: e.matmul(out, lhsT=lhsT, rhs=rhs, start=start, stop=stop), reads, writes)

        def tr(out, in_, ident, reads, writes):
            P.emit('pe', lambda e: e.transpose(out, in_, ident), reads, writes)

        def act(out, in_, func, reads, writes, scale=1.0, bias=0.0, accum=None):
            if accum is None:
                P.emit('act', lambda e: e.activation(out=out, in_=in_, func=func, bias=bias, scale=scale), reads, writes)
            else:
                P.emit('act', lambda e: e.activation(out=out, in_=in_, func=func, bias=bias, scale=scale,
                                                     accum_out=accum), reads, writes)

        def tt(eng, out, in0, in1, op, reads, writes):
            P.emit(eng, lambda e: e.tensor_tensor(out=out, in0=in0, in1=in1, op=op), reads, writes)

        def ts(eng, out, in0, s1, op0, reads, writes, s2=None, op1=None):
            if op1 is None:
                P.emit(eng, lambda e: e.tensor_scalar(out=out, in0=in0, scalar1=s1, scalar2=None, op0=op0), reads, writes)
            else:
                P.emit(eng, lambda e: e.tensor_scalar(out=out, in0=in0, scalar1=s1, scalar2=s2, op0=op0, op1=op1),
                       reads, writes)

        def stt(eng, out, in0, scalar, in1, op0, op1, reads, writes):
            P.emit(eng, lambda e: e.scalar_tensor_tensor(out=out, in0=in0, scalar=scalar, in1=in1, op0=op0, op1=op1),
                   reads, writes)

        def cp(eng, out, in_, reads, writes):
            if eng == 'act':
                P.emit('act', lambda e: e.copy(out=out, in_=in_), reads, writes)
            else:
                P.emit(eng, lambda e: e.tensor_copy(out=out, in_=in_), reads, writes)

        def mset(eng, ap, val, writes):
            P.emit(eng, lambda e: e.memset(ap, val), (), writes)

        def recip(out, in_, reads, writes):
            P.emit('dve', lambda e: e.reciprocal(out=out, in_=in_), reads, writes)

        def rstd_ops(ssq_ap, lnt_ap, rstd_ap, n, cell):
            act(lnt_ap, ssq_ap, AF.Ln, [cell], [cell], scale=1.0 / n, bias=EPS)
            act(rstd_ap, lnt_ap, AF.Exp, [cell], [cell], scale=-0.5)

        with ExitStack() as sAB:
            def sb(name, shape, dt):
                return sAB.enter_context(nc.sbuf_tensor(name, list(shape), dt))
            ident = sb("ident", [128, 128], BF16)
            w_insub = sb("w_insub", [128, 8, 800], BF16)
            w_uqn_sb = sb("w_uqn_sb", [128, 4, 1024], BF16)
            w_uqr_sb = sb("w_uqr_sb", [128, 4, 512], BF16)
            w_uqs_sb = sb("w_uqs_sb", [128, 4, 512], BF16)
            w_ukv_sb = sb("w_ukv_sb", [128, 2, 2048], BF16)
            gq_b = sb("gq_b", [128, 512], F32)
            gkv_b = sb("gkv_b", [128, 256], F32)
            gmix_c = sb("gmix_c", [128, 8], F32)
            ckvT_p = sb("ckvT_p", [128, 2, T], BF16)
            ckvT_s = sb("ckvT_s", [128, 2, PAST + TS], BF16)
            Kb = [sb("Kb%d" % i, [96, T], BF16) for i in range(2)]
            Ks = sb("Ks", [96, PAST + TS], BF16)

            dma('pool', ident[:], identf, [], ['ident'])
            dma('pool', w_insub[:, 0:4, :], w_in[0:512, 3072:3872].rearrange("(kc p) c -> p kc c", p=128), [], [('w_insub', 0)])
            dma('pool', w_insub[:, 4:8, :], w_in[512:1024, 3072:3872].rearrange("(kc p) c -> p kc c", p=128), [], [('w_insub', 1)])
            dma('pool', w_uqn_sb[:], w_uqn.rearrange("(kc p) c -> p kc c", p=128), [], ['w_uq'])
            dma('pool', w_uqr_sb[:], w_uqr.rearrange("(kc p) c -> p kc c", p=128), [], ['w_uq'])
            dma('pool', w_uqs_sb[:], w_uqs.rearrange("(kc p) c -> p kc c", p=128), [], ['w_uqs'])
            dma('pool', w_ukv_sb[:], w_ukv.rearrange("(kc p) c -> p kc c", p=128), [], ['w_ukv'])
            dma('sp', gq_b[:], g_q.partition_broadcast(128), [], ['gq_b'], odd=True)
            dma('sp', gkv_b[:], g_kv.partition_broadcast(128), [], ['gkv_b'], odd=True)
            dma('sp', gmix_c[:], gcols_h[:, 0:8], [], ['gmix_c'], odd=True)

            with ExitStack() as sA:
                def sa(name, shape, dt):
                    return sA.enter_context(nc.sbuf_tensor(name, list(shape), dt))
                XT = [sa("xt%d" % i, [128, 4, D], F32) for i in range(2)]
                XN = [sa("xn%d" % i, [128, D], BF16) for i in range(4)]
                junk = sa("junkA", [128, D], BF16)
                nT = sa("nT", [128, 8, 512], BF16)
                CQN = [sa("cqn%d" % i, [128, 512], BF16) for i in range(2)]
                CQNT = [sa("cqnT%d" % i, [128, 4, 512], BF16) for i in range(2)]
                CKVO = [sa("ckvo%d" % i, [128, 4, 256], F32) for i in range(2)]
                CKVB = [sa("ckvb%d" % i, [128, 256], BF16) for i in range(2)]
                KRO = [sa("kro%d" % i, [128, 4, 32], F32) for i in range(2)]
                KRP = [sa("krp%d" % i, [128, 96], BF16) for i in range(2)]
                RTM = [sa("rtm%d" % i, [128, 4, 64], F32) for i in range(2)]
                RFC = [sa("rfc%d" % i, [128, 512], F32) for i in range(2)]
                RFS = [sa("rfs%d" % i, [128, 512], F32) for i in range(2)]
                QN = [sa("qn%d" % i, [128, 512], BF16) for i in range(2)]
                QR = [sa("qr%d" % i, [128, 512], BF16) for i in range(2)]
                T1 = [sa("t1_%d" % i, [128, 512], F32) for i in range(2)]
                T2 = [sa("t2_%d" % i, [128, 512], F32) for i in range(2)]
                TA = [sa("ta%d" % i, [128, 32], F32) for i in range(2)]
                TB = [sa("tb%d" % i, [128, 32], F32) for i in range(2)]
                STAT = [sa("stat%d" % i, [128, 36], F32) for i in range(2)]
                cckv_bf = sa("cckv_bf", [128, 8, 256], BF16)
                ckr_pad = sa("ckr_pad", [128, 8, 96], BF16)
                psf = [sA.enter_context(nc.psum_tensor("psfA%d" % i, [128, 512], F32)) for i in range(6)]
                psb = [sA.enter_context(nc.psum_tensor("psbA%d" % i, [128, 1024], BF16)) for i in range(2)]
                PF = RR([(psf[i], 'psf%d' % i) for i in range(6)])
                PB = RR([(psb[i], 'psb%d' % i) for i in range(2)])

                dma('pool', cckv_bf[:], cckv.rearrange("(t p) c -> p t c", p=128), [], ['cckv_bf'])
                mset('dve', ckr_pad[:], 0.0, ['ckr_pad'])
                mset('dve', KRP[0][:], 0.0, ['krp0'])
                mset('dve', KRP[1][:], 0.0, ['krp1'])
                dma('pool', ckr_pad[:, :, 64:96], ckr.rearrange("(t p) c -> p t c", p=128), [], ['ckr_pad'], odd=True)

                evi = [0]
                def evac_copy(out, in_, reads, writes):
                    eng = 'act' if evi[0] % 2 == 0 else 'dve'
                    evi[0] += 1
                    cp(eng, out, in_, reads, writes)

                def sample_past():
                    for t in range(8):
                        pt_, pc = PB.get()
                        for c in range(2):
                            tr(pt_[:, c * 128:(c + 1) * 128], cckv_bf[:, t, c * 128:(c + 1) * 128], ident[:, :],
                               ['cckv_bf', 'ident'], [pc])
                        evac_copy(ckvT_s[:, :, t * 128:(t + 1) * 128],
                                  pt_[:, 0:256].rearrange("p (c t) -> p c t", c=2), [pc], [('ckvT_s', t)])
                        pt2, pc2 = PB.get()
                        tr(pt2[0:96, 0:128], ckr_pad[:, t, :], ident[:, :], ['ckr_pad', 'ident'], [pc2])
                        evac_copy(Ks[64:96, t * 128:(t + 1) * 128], pt2[64:96, 0:128], [pc2], [('Ks_r', t)])

                def load_tile(it, x_rows, TT, rt0):
                    nsub = max(1, TT // 128); np_ = min(TT, 128)
                    b = it % 2
                    dma('sp', XT[b][0:np_, 0:nsub, :], x_rows.rearrange("(s p) d -> p s d", p=np_), [], [('xt', b)])
                    dma('sp', RTM[b][0:np_, 0:nsub, :], rt_tm[rt0:rt0 + TT, :].rearrange("(s p) c -> p s c", p=np_),
                        [], [('rtm', b)])

                def load_rf(it, TT, rt0):
                    b = it % 2
                    dma('sp', RFC[b][:, 0:TT], rt_c[:, rt0:rt0 + TT], [], [('rfc', b)])
                    dma('sp', RFS[b][:, 0:TT], rt_s[:, rt0:rt0 + TT], [], [('rfs', b)])

                def nrm_a(it, TT, s):
                    np_ = min(TT, 128); b = it % 2
                    xt = XT[b]; st = STAT[b]; stc = ('stat', b)
                    act(junk[0:np_, :], xt[0:np_, s, :], AF.Square, [('xt', b), stc], ['junkA', stc],
                        accum=st[0:np_, s:s + 1])
                    rstd_ops(st[0:np_, s:s + 1], st[0:np_, 12 + s:13 + s], st[0:np_, 24 + s:25 + s], float(D), stc)
                    ts('dve', XN[s % 4][0:np_, :], xt[0:np_, s, :], st[0:np_, 24 + s:25 + s], ALU.mult,
                       [('xt', b), stc], [('xn', s % 4)])

                def nrm_b(TT, s):
                    np_ = min(TT, 128)
                    xn = XN[s % 4]; xnc = ('xn', s % 4)
                    pt_, pc = PB.get()
                    for c in range(8):
                        tr(pt_[:, c * 128:c * 128 + np_], xn[0:np_, c * 128:(c + 1) * 128], ident[0:np_, 0:np_],
                           [xnc, 'ident'], [pc])
                    tt('dve', nT[:, :, s * 128:s * 128 + np_],
                       pt_[:, :].rearrange("p (c t) -> p c t", c=8)[:, :, 0:np_],
                       gmix_c[:, 0:8].unsqueeze(2).broadcast_to([128, 8, np_]), ALU.mult,
                       [pc, 'gmix_c'], [('nT', s)])

                def norm_full(it, TT):
                    nsub = max(1, TT // 128)
                    mset('dve', STAT[it % 2][:, 0:12], 0.0, [('stat', it % 2)])
                    for s in range(nsub):
                        nrm_a(it, TT, s)
                    for s in range(nsub):
                        nrm_b(TT, s)

                def phaseA_tile(it, TT, ckvT_dst, ckvT_name, ckv_col0, kr_dsts, out_ckv, out_kr, qdst, pending_q,
                                nxt):
                    nsub = max(1, TT // 128); np_ = min(TT, 128)
                    b = it % 2
                    st = STAT[b]; stc = ('stat', b)
                    cqnT = CQNT[b]
                    if nxt is not None:
                        mset('dve', STAT[nxt[0] % 2][:, 0:12], 0.0, [('stat', nxt[0] % 2)])
                        nsub_n = max(1, nxt[1] // 128)
                        nxt_a = [(lambda s_: (lambda: nrm_a(nxt[0], nxt[1], s_)))(s_) for s_ in range(nsub_n)]
                    else:
                        nxt_a = []
                    held = {}

                    def part_a(s):
                        psA, cA = PF.get()
                        psB, cB = PF.get()
                        for kc in range(8):
                            mm(psA[0:np_, 0:512], nT[:, kc, s * 128:s * 128 + np_], w_insub[:, kc, 0:512],
                               kc == 0, kc == 7, [('nT', s), ('w_insub', 0), ('w_insub', 1)], [cA])
                        for kc in range(8):
                            mm(psB[0:np_, 0:288], nT[:, kc, s * 128:s * 128 + np_], w_insub[:, kc, 512:800],
                               kc == 0, kc == 7, [('nT', s), ('w_insub', 0), ('w_insub', 1)], [cB])
                        if nsub == 1:
                            while pending_q:
                                pending_q.pop(0)()
                        elif pending_q:
                            pending_q.pop(0)()
                        act(junk[0:np_, 0:512], psA[0:np_, 0:512], AF.Square, [cA, stc], ['junkA', stc],
                            accum=st[0:np_, 4 + s:5 + s])
                        act(junk[0:np_, 0:256], psB[0:np_, 0:256], AF.Square, [cB, stc], ['junkA', stc],
                            accum=st[0:np_, 8 + s:9 + s])
                        rstd_ops(st[0:np_, 4 + s:5 + s], st[0:np_, 16 + s:17 + s], st[0:np_, 28 + s:29 + s], 512.0, stc)
                        rstd_ops(st[0:np_, 8 + s:9 + s], st[0:np_, 20 + s:21 + s], st[0:np_, 32 + s:33 + s], 256.0, stc)
                        cqn = CQN[s % 2]; cqc = ('cqn', s % 2)
                        stt('dve', cqn[0:np_, :], psA[0:np_, 0:512], st[0:np_, 28 + s:29 + s], gq_b[0:np_, :],
                            ALU.mult, ALU.mult, [cA, stc, 'gq_b'], [cqc])
                        ckvo = CKVO[b]
                        stt('dve', ckvo[0:np_, s, :], psB[0:np_, 0:256], st[0:np_, 32 + s:33 + s], gkv_b[0:np_, :],
                            ALU.mult, ALU.mult, [cB, stc, 'gkv_b'], [('ckvo', b, s)])
                        ckvb = CKVB[s % 2]; ckc = ('ckvb', s % 2)
                        cp('act', ckvb[0:np_, :], ckvo[0:np_, s, :], [('ckvo', b, s)], [ckc])
                        ta = TA[s % 2]; tb = TB[s % 2]; tac = ('ta', s % 2)
                        tt('dve', ta[0:np_, :], psB[0:np_, 256:288], RTM[b][0:np_, s, 0:32], ALU.mult,
                           [cB, ('rtm', b)], [tac])
                        tt('dve', tb[0:np_, :], psB[0:np_, 256:288], RTM[b][0:np_, s, 32:64], ALU.mult,
                           [cB, ('rtm', b)], [tac])
                        kro = KRO[b]
                        tt('dve', kro[0:np_, s, 0:16], ta[0:np_, 0:16], tb[0:np_, 16:32], ALU.subtract,
                           [tac], [('kro', b, s)])
                        tt('dve', kro[0:np_, s, 16:32], ta[0:np_, 16:32], tb[0:np_, 0:16], ALU.add,
                           [tac], [('kro', b, s)])
                        krp = KRP[s % 2]; krc = 'krp%d' % (s % 2)
                        cp('act', krp[0:np_, 64:96], kro[0:np_, s, :], [('kro', b, s)], [krc])
                        if nsub == 1:
                            while nxt_a:
                                nxt_a.pop(0)()
                        elif nxt_a:
                            nxt_a.pop(0)()

                    def part_b(s):
                        cqn = CQN[s % 2]; cqc = ('cqn', s % 2)
                        pt_, pc = PB.get()
                        for c in range(4):
                            tr(pt_[:, c * 128:c * 128 + np_], cqn[0:np_, c * 128:(c + 1) * 128], ident[0:np_, 0:np_],
                               [cqc, 'ident'], [pc])
                        ckvb = CKVB[s % 2]; ckc = ('ckvb', s % 2)
                        for c in range(2):
                            tr(pt_[:, 512 + c * 128:512 + c * 128 + np_], ckvb[0:np_, c * 128:(c + 1) * 128],
                               ident[0:np_, 0:np_], [ckc, 'ident'], [pc])
                        krp = KRP[s % 2]; krc = 'krp%d' % (s % 2)
                        tr(pt_[0:96, 768:768 + np_], krp[0:np_, :], ident[0:np_, 0:np_], [krc, 'ident'], [pc])
                        cp('act', cqnT[:, :, s * 128:s * 128 + np_],
                           pt_[:, 0:512].rearrange("p (c t) -> p c t", c=4)[:, :, 0:np_], [pc], [('cqnT', b, s)])
                        col = ckv_col0 + s * 128
                        cp('dve', ckvT_dst[:, :, col:col + np_],
                           pt_[:, 512:768].rearrange("p (c t) -> p c t", c=2)[:, :, 0:np_], [pc],
                           [(ckvT_name, col // 128)])
                        for (dst, dname, c0) in kr_dsts:
                            col = c0 + s * 128
                            evac_copy(dst[64:96, col:col + np_], pt_[64:96, 768:768 + np_], [pc], [(dname, col // 128)])

                    if dbg_stage < 2:
                        return []
                    for s in range(nsub):
                        part_a(s)
                        if s >= 1:
                            part_b(s - 1)
                    part_b(nsub - 1)
                    while nxt_a:
                        nxt_a.pop(0)()
                    if nxt is not None:
                        for s_ in range(max(1, nxt[1] // 128)):
                            nrm_b(nxt[1], s_)
                    if dbg_stage < 3:
                        return []
                    dma('sp', out_ckv.rearrange("(s p) c -> p s c", p=np_), CKVO[b][0:np_, 0:nsub, :],
                        [('ckvo', b, s) for s in range(nsub)], [])
                    dma('sp', out_kr.rearrange("(s p) c -> p s c", p=np_), KRO[b][0:np_, 0:nsub, :],
                        [('kro', b, s) for s in range(nsub)], [], odd=True)
                    cq_cells = [('cqnT', b, s) for s in range(nsub)]
                    if dbg_stage < 4:
                        return []

                    def q_nope(pr):
                        ps1, c1 = PF.get()
                        for kc in range(4):
                            mm(ps1[:, 0:TT], w_uqn_sb[:, kc, pr * 128:(pr + 1) * 128], cqnT[:, kc, 0:TT],
                               kc == 0, kc == 3, cq_cells + ['w_uq'], [c1])
                        qn = QN[pr % 2]; qnc = ('qn', pr % 2)
                        cp('act', qn[:, 0:TT], ps1[:, 0:TT], [c1], [qnc])
                        dma('sp', qdst('n', pr), qn[:, 0:TT], [qnc], [('qscn', pr)])

                    def q_rope(qd):
                        ps1, c1 = PF.get()
                        ps2, c2 = PF.get()
                        for kc in range(4):
                            mm(ps1[:, 0:TT], w_uqr_sb[:, kc, qd * 128:(qd + 1) * 128], cqnT[:, kc, 0:TT],
                               kc == 0, kc == 3, cq_cells + ['w_uq'], [c1])
                        for kc in range(4):
                            mm(ps2[:, 0:TT], w_uqs_sb[:, kc, qd * 128:(qd + 1) * 128], cqnT[:, kc, 0:TT],
                               kc == 0, kc == 3, cq_cells + ['w_uqs'], [c2])
                        t1 = T1[qd % 2]; t2 = T2[qd % 2]
                        tt('dve', t1[:, 0:TT], ps1[:, 0:TT], RFC[b][:, 0:TT], ALU.mult, [c1, ('rfc', b)], [('t1', qd % 2)])
                        tt('dve', t2[:, 0:TT], ps2[:, 0:TT], RFS[b][:, 0:TT], ALU.mult, [c2, ('rfs', b)], [('t2', qd % 2)])
                        qr = QR[qd % 2]; qrc = ('qr', qd % 2)
                        tt('dve', qr[:, 0:TT], t1[:, 0:TT], t2[:, 0:TT], ALU.add, [('t1', qd % 2), ('t2', qd % 2)], [qrc])
                        dma('sp', qdst('r', qd), qr[:, 0:TT], [qrc], [('qscr', qd)])

                    def part(k):
                        def f():
                            q_nope(2 * k)
                            q_nope(2 * k + 1)
                            q_rope(k)
                        return f
                    return [part(k) for k in range(4)]

                tiles = [dict(x=x_s, TT=TS, rt0=T)]
                for tt_i in range(8):
                    tiles.append(dict(x=x_p[tt_i * 512:(tt_i + 1) * 512, :], TT=512, rt0=tt_i * 512))
                tiles = tiles[:dbg_tiles]
                load_tile(0, tiles[0]['x'], tiles[0]['TT'], tiles[0]['rt0'])
                load_rf(0, tiles[0]['TT'], tiles[0]['rt0'])
                norm_full(0, tiles[0]['TT'])
                pending = []
                for it, td in enumerate(tiles):
                    if it + 1 < len(tiles):
                        nx = tiles[it + 1]
                        load_tile(it + 1, nx['x'], nx['TT'], nx['rt0'])
                    nxt = (it + 1, tiles[it + 1]['TT']) if it + 1 < len(tiles) else None
                    if it == 0:
                        newq = phaseA_tile(0, TS, ckvT_s, 'ckvT_s', PAST, [(Ks, 'Ks_r', PAST)], nckv_s, nkr_s,
                                           lambda k, i: (qscn_s if k == 'n' else qscr_s)[i], pending, nxt)
                        sample_past()
                    else:
                        ti = it - 1
                        newq = phaseA_tile(it, 512, ckvT_p, 'ckvT_p', ti * 512,
                                           [(Kb[0], 'Kb0_r', ti * 512), (Kb[1], 'Kb1_r', ti * 512)],
                                           nckv_p[ti * 512:(ti + 1) * 512, :], nkr_p[ti * 512:(ti + 1) * 512, :],
                                           (lambda ti_: (lambda k, i: (qscn if k == 'n' else qscr)[i][:, ti_ * 512:(ti_ + 1) * 512]))(ti),
                                           pending, nxt)
                    while pending:
                        pending.pop(0)()
                    pending = list(newq)
                    if it + 1 < len(tiles):
                        nx = tiles[it + 1]
                        load_rf(it + 1, nx['TT'], nx['rt0'])
                while pending:
                    pending.pop(0)()
                if 'A' in phases:
                    P.run(final=(phases == 'A'))
                else:
                    P._reset()
            with ExitStack() as sB:
                def sbb(name, shape, dt):
                    return sB.enter_context(nc.sbuf_tensor(name, list(shape), dt))
                Vb = [sbb("Vb%d" % i, [128, 32, 128], BF16) for i in range(2)]
                Vs = sbb("Vs", [128, 9, 128], BF16)
                Qb = [sbb("Qb%d" % i, [96, T], BF16) for i in range(2)]
                Qs_sb = sbb("Qs_sb", [96, TS], BF16)
                PT = [sbb("PT%d" % i, [128, 1024], BF16) for i in range(4)]
                RC = [sbb("RC%d" % i, [64, 512], F32) for i in range(2)]
                OT = [sbb("OT%d" % i, [64, 512], BF16) for i in range(2)]
                psS = [sB.enter_context(nc.psum_tensor("psS%d" % i, [128, 1024], F32)) for i in range(3)]
                psAcc = [sB.enter_context(nc.psum_tensor("psAcc%d" % i, [128, 512], F32)) for i in range(2)]
                SP_ = RR([(psS[i], 'psS%d' % i) for i in range(3)])
                AP_ = RR([(psAcc[i], 'psAcc%d' % i) for i in range(2)])
                GP_ = SP_
                PTR = RR([(PT[i], 'PT%d' % i) for i in range(4)])
                ORR = RR([(RC[i], OT[i], 'RCOT%d' % i) for i in range(2)])
                mset('dve', Vb[0][:, :, 64:128], 1.0, ['Vb0'])
                mset('dve', Vb[1][:, :, 64:128], 1.0, ['Vb1'])
                mset('dve', Vs[:, :, 64:128], 1.0, ['Vs'])
                conv_list = list(pan_spec.values()) if conv else []

                def emit_conv(k, paced_on):
                    for (pi, src, r0, c0) in conv_list[k:k + 3]:
                        dma('pool', wsc[pi].rearrange("p (kc c) -> p kc c", kc=8),
                            src[r0:r0 + 1024, c0:c0 + 512].rearrange("(kc p) c -> p kc c", p=128),
                            paced_on, [('wsc', pi)], carry=True)
                gi = [0]

                def gen_head(h, Kbuf, kcell, Vbuf, vcell, ckvT, L, Qbuf, qcell, qsrc, NQ):
                    pieces = []

                    def kpiece(kb):
                        def f():
                            n = min(512, L - kb * 512)
                            ps, pc = GP_.get()
                            for kc in range(2):
                                mm(ps[0:64, 0:n], w_ukv_sb[:, kc, h * 128:h * 128 + 64],
                                   ckvT[:, kc, kb * 512:kb * 512 + n], kc == 0, kc == 1, ['w_ukv'], [pc])
                            cp('dve', Kbuf[0:64, kb * 512:kb * 512 + n], ps[0:64, 0:n], [pc], [(kcell, kb)])
                        return f

                    def vpiece(g0):
                        def f():
                            nkt = (L + 127) // 128
                            ps, pc = GP_.get()
                            full = True
                            cnt = min(8, nkt - g0)
                            for j in range(cnt):
                                kt = g0 + j
                                nk = min(128, L - kt * 128)
                                if nk < 128:
                                    full = False
                                for kc in range(2):
                                    mm(ps[0:nk, j * 64:(j + 1) * 64], ckvT[:, kc, kt * 128:kt * 128 + nk],
                                       w_ukv_sb[:, kc, h * 128 + 64:h * 128 + 128], kc == 0, kc == 1, ['w_ukv'], [pc])
                            npar = 128 if full else min(128, L - g0 * 128)
                            cp('dve', Vbuf[0:npar, g0:g0 + cnt, 0:64],
                               ps[0:npar, 0:cnt * 64].rearrange("p (j d) -> p j d", d=64), [pc], [(vcell, g0 // 8)])
                        return f

                    qn_, qr_ = qsrc
                    dma('sp', Qbuf[0:64, 0:NQ], qn_[h // 2][(h % 2) * 64:(h % 2) * 64 + 64, :], [], [qcell])
                    dma('sp', Qbuf[64:96, 0:NQ], qr_[h // 4][(h % 4) * 32:(h % 4) * 32 + 32, :], [], [(qcell, 'r')])
                    nkb = (L + 511) // 512
                    nkt = (L + 127) // 128
                    kps = [kpiece(kb) for kb in range(nkb)]
                    vps = [vpiece(g0) for g0 in range(0, nkt, 8)]
                    while kps or vps:
                        for _ in range(2):
                            if kps:
                                pieces.append(kps.pop(0))
                        if vps:
                            pieces.append(vps.pop(0))
                    return pieces

                LA = 2

                def attn_head(h, Kbuf, kcell, krcells, Vbuf, vcell, Qbuf, qcell, blocks, odst, pieces=()):
                    items = []
                    for bi, (q0, nq, tl) in enumerate(blocks):
                        k = 0
                        while k < len(tl):
                            if k + 1 < len(tl) and tl[k][1] == 128 and tl[k + 1][1] == 128:
                                grp = [tl[k], tl[k + 1]]
                                k += 2
                            else:
                                grp = [tl[k]]
                                k += 1
                            items.append((bi, q0, nq, len(tl), grp))
                    accs = {}
                    pend = {}
                    done = {}

                    def stage1(i):
                        (bi, q0, nq, n, grp) = items[i]
                        st, sc = SP_.get()
                        pt_, pc = PTR.get()
                        c0p = grp[0][2]
                        for hf, (kt, nk, c0, diag) in enumerate(grp):
                            mm(st[0:nk, hf * 512 + c0p:hf * 512 + nq], Kbuf[0:96, kt * 128:kt * 128 + nk],
                               Qbuf[0:96, q0 + c0p:q0 + nq], True, True,
                               [(kcell, kt // 4), qcell, (qcell, 'r')] + krcells, [sc])
                        if len(grp) == 2:
                            act(pt_[:, :].rearrange("p (h c) -> p h c", h=2)[:, :, c0p:nq],
                                st[:, :].rearrange("p (h c) -> p h c", h=2)[:, :, c0p:nq], AF.Exp, [sc], [pc],
                                scale=ATTN_SCALE)
                        else:
                            nk = grp[0][1]
                            act(pt_[0:nk, c0p:nq], st[0:nk, c0p:nq], AF.Exp, [sc], [pc], scale=ATTN_SCALE)
                        for hf, (kt, nk, c0, diag) in enumerate(grp):
                            if diag:
                                mset('dve', pt_[64:128, hf * 512 + c0:hf * 512 + c0 + 64], 0.0, [pc])
                        pend[i] = (pt_, pc)

                    def stage2(i):
                        (bi, q0, nq, n, grp) = items[i]
                        if bi not in accs:
                            accs[bi] = AP_.get()
                            done[bi] = 0
                        acc, ac = accs[bi]
                        pt_, pc = pend.pop(i)
                        for hf, (kt, nk, c0, diag) in enumerate(grp):
                            first = done[bi] == 0
                            done[bi] += 1
                            last = done[bi] == n
                            mm(acc[:, c0:nq], Vbuf[0:nk, kt, :], pt_[0:nk, hf * 512 + c0:hf * 512 + nq], first, last,
                               [(vcell, kt // 8), vcell, pc], [ac])
                        if done[bi] == n:
                            rc, ot, oc = ORR.get()
                            recip(rc[0:64, 0:nq], acc[64:128, 0:nq], [ac], [oc])
                            tt('dve', ot[0:64, 0:nq], acc[0:64, 0:nq], rc[0:64, 0:nq], ALU.mult, [ac, oc], [oc])
                            dma('sp', odst(q0, nq), ot[0:64, 0:nq], [oc], [('osc', h)])

                    pieces = list(pieces)
                    every = max(1, (len(items) - 4) // (len(pieces) + 1)) if pieces else 0
                    for i in range(len(items) + LA):
                        if i < len(items):
                            stage1(i)
                        if i >= LA:
                            stage2(i - LA)
                        if pieces and i > 0 and i % every == 0:
                            pieces.pop(0)()
                    while pieces:
                        pieces.pop(0)()

                pblocks = []
                for qb in range(8):
                    tl = [(kt, 128, 0, False) for kt in range(4 * qb)]
                    tl += [(4 * qb + j, 128, 128 * j, True) for j in range(4)]
                    pblocks.append((qb * 512, 512, tl))
                sblocks = [(0, TS, [(kt, 128, 0, False) for kt in range(8)] + [(8, 64, 0, False)])]
                krc_p = [[('Kb%d_r' % i, c) for c in range(32)] for i in range(2)]
                krc_s = [('Ks_r', c) for c in range(9)]

                def odst_p(h):
                    return lambda q0, nq: osc[h // 2][(h % 2) * 64:(h % 2) * 64 + 64, q0:q0 + nq]

                def odst_s(h):
                    return lambda q0, nq: osc_s[h // 2][(h % 2) * 64:(h % 2) * 64 + 64, q0:q0 + nq]

                for h in range(NH):
                    for pc_ in gen_head(h, Ks, 'Ks_n', Vs, 'Vs', ckvT_s, PAST + TS, Qs_sb, 'Qs_sb', (qscn_s, qscr_s), TS):
                        pc_()
                    nxt = ()
                    if h == NH - 1:
                        nxt = gen_head(0, Kb[0], 'Kb0_n', Vb[0], 'Vb0', ckvT_p, T, Qb[0], 'Qb0', (qscn, qscr), T)
                    attn_head(h, Ks, 'Ks_n', krc_s, Vs, 'Vs', Qs_sb, 'Qs_sb', sblocks, odst_s(h), nxt)
                for h in range(NH):
                    i = h % 2
                    nxt = ()
                    if h + 1 < NH:
                        j = (h + 1) % 2
                        nxt = gen_head(h + 1, Kb[j], 'Kb%d_n' % j, Vb[j], 'Vb%d' % j, ckvT_p, T, Qb[j], 'Qb%d' % j,
                                       (qscn, qscr), T)
                    attn_head(h, Kb[i], 'Kb%d_n' % i, krc_p[i], Vb[i], 'Vb%d' % i, Qb[i], 'Qb%d' % i, pblocks,
                              odst_p(h), nxt)
                    emit_conv(3 * h, [('osc', h)])
                if 'B' in phases:
                    P.run(final=('C' not in phases))
                else:
                    P._reset()

        with ExitStack() as sC:
            def sc_(name, shape, dt):
                return sC.enter_context(nc.sbuf_tensor(name, list(shape), dt))
            NB = 5
            WR = [sc_("wr%d" % i, [128, 8, 512], BF16) for i in range(NB)]
            ident = sc_("identC", [128, 128], BF16)
            gcols = sc_("gcols", [128, 4, 8], F32)
            gfin_b = sc_("gfin_b", [128, D], F32)
            wconv_c = sc_("wconv_c", [128, 3, 8], F32)
            XTB = [sc_("XTm%d" % i, [128, 4, D], F32) for i in range(2)]
            XN = [sc_("xnC%d" % i, [128, D], BF16) for i in range(4)]
            junk = sc_("junkC", [128, D], BF16)
            actT = sc_("actT", [128, 8, 512], BF16)
            oT = sc_("oT", [128, 8, 512], BF16)
            wT = sc_("wT", [128, 8, 512], BF16)
            mT = sc_("mT", [128, 8, 512], BF16)
            VJ = [sc_("vj%d" % i, [128, 514], F32) for i in range(2)]
            TMPF = [sc_("tmpf%d" % i, [128, 512], F32) for i in range(4)]
            halo = sc_("halo", [128, 8, 2], F32)
            qmT = sc_("qmT", [128, 8, 512], BF16)
            PTm = [sc_("ptm%d" % i, [128, 2, 512], BF16) for i in range(2)]
            omT = sc_("omT", [128, 8, 512], BF16)
            memKT = sc_("memKT", [128, 8, NMEM], BF16)
            memV = sc_("memV", [128, 2, D], BF16)
            memtmp = sc_("memtmp", [128, 2, D], F32)
            hid = sc_("hid", [128, 32, 512], BF16)
            membf = hid[:, 0:4, :].rearrange("p (s a) b -> p s (a b)", s=2)
            ones = sc_("ones", [128, 128], BF16)
            STAT = sc_("statC", [128, 48], F32)
            psf = [sC.enter_context(nc.psum_tensor("psfC%d" % i, [128, 512], F32)) for i in range(6)]
            psb = [sC.enter_context(nc.psum_tensor("psbC%d" % i, [128, 1024], BF16)) for i in range(2)]
            PF = RR([(psf[i], 'psf%d' % i) for i in range(6)])
            PB = RR([(psb[i], 'psb%d' % i) for i in range(2)])

            dma('pool', ident[:], identf, [], ['ident'])
            dma('sp', gcols[:], gcols_h.rearrange("p (g c) -> p g c", g=4), [], ['gcols'], odd=True)
            dma('sp', gfin_b[:], g_final.partition_broadcast(128), [], ['gfin_b'], odd=True)
            dma('sp', wconv_c[:], wconv_h.rearrange("p (k c) -> p k c", k=3), [], ['wconv_c'], odd=True)
            mset('dve', ones[:], 1.0, ['ones'])

            wri = [0]

            def load_panel(name):
                pi = pan_spec[name][0]
                k = wri[0] % NB
                wri[0] += 1
                dma('sp', WR[k][:], wsc[pi].rearrange("p (kc c) -> p kc c", kc=8), [('wsc', pi)], [('wr', k)])
                return WR[k], ('wr', k)

            def norm_a(src_ap, src_cells, s, np_, scol):
                stc = ('stat', scol)
                act(junk[0:np_, :], src_ap, AF.Square, src_cells + [stc], ['junkC', stc],
                    accum=STAT[0:np_, scol + s:scol + s + 1])
                rstd_ops(STAT[0:np_, scol + s:scol + s + 1], STAT[0:np_, scol + 4 + s:scol + 5 + s],
                         STAT[0:np_, scol + 8 + s:scol + 9 + s], float(D), stc)
                ts('dve', XN[s % 4][0:np_, :], src_ap, STAT[0:np_, scol + 8 + s:scol + 9 + s], ALU.mult,
                   src_cells + [stc], [('xn', s % 4)])

            def norm_b(s, np_, gidx, dstT, dst_cell):
                xn = XN[s % 4]; xnc = ('xn', s % 4)
                pt_, pc = PB.get()
                for c in range(8):
                    tr(pt_[:, c * 128:c * 128 + np_], xn[0:np_, c * 128:(c + 1) * 128], ident[0:np_, 0:np_],
                       [xnc, 'ident'], [pc])
                tt('dve', dstT[:, :, s * 128:s * 128 + np_],
                   pt_[:, :].rearrange("p (c t) -> p c t", c=8)[:, :, 0:np_],
                   gcols[:, gidx, :].unsqueeze(2).broadcast_to([128, 8, np_]), ALU.mult,
                   [pc, 'gcols'], [(dst_cell, s)])

            def norm_T(src_fn, np_, nsub, gidx, dstT, dst_cell, src_cells, scol):
                mset('dve', STAT[:, scol:scol + nsub], 0.0, [('stat', scol)])
                for s in range(nsub):
                    norm_a(src_fn(s), src_cells, s, np_, scol)
                    if s >= 1:
                        norm_b(s - 1, np_, gidx, dstT, dst_cell)
                norm_b(nsub - 1, np_, gidx, dstT, dst_cell)

            def prep_mem_prompt():
                dma('sp', memtmp[:], memp.rearrange("(s p) d -> p s d", p=128), [], ['memtmp'])
                norm_T(lambda s: memtmp[:, s, :], 128, 2, 3, actT, 'actT', ['memtmp'], 12)
                for (pfx, dst_out, is_k) in (('km', nmk_p, True), ('vm', nmv_p, False)):
                    for half in range(2):
                        wp, wc = load_panel('%s%d' % (pfx, half))
                        for s in range(2):
                            ps, pc = PF.get()
                            for kc in range(8):
                                mm(ps[:, :], actT[:, kc, s * 128:(s + 1) * 128], wp[:, kc, :], kc == 0, kc == 7,
                                   [('actT', s), wc], [pc])
                            cp('act', memtmp[:, s, half * 512:(half + 1) * 512], ps[:, :], [pc], ['memtmp'])
                    dma('pool', dst_out.rearrange("(s p) d -> p s d", p=128), memtmp[:], ['memtmp'], [])
                    if is_k:
                        cp('dve', membf, memtmp[:], ['memtmp'], ['membf'] + [('hid', c_) for c_ in range(4)])
                        for s in range(2):
                            pt_, pc = PB.get()
                            for c in range(8):
                                tr(pt_[:, c * 128:(c + 1) * 128], membf[:, s, c * 128:(c + 1) * 128], ident[:, :],
                                   ['membf', 'ident'], [pc])
                            cp('dve', memKT[:, :, s * 128:(s + 1) * 128],
                               pt_[:, :].rearrange("p (c t) -> p c t", c=8), [pc], ['memKT'])
                    else:
                        cp('dve', memV[:], memtmp[:], ['memtmp'], ['memV'])

            def prep_mem_sample():
                dma('pool', membf, cmk.rearrange("(s p) d -> p s d", p=128), [], ['membf'] + [('hid', c_) for c_ in range(4)])
                dma('pool', memV[:], cmv.rearrange("(s p) d -> p s d", p=128), [], ['memV'])
                for s in range(2):
                    pt_, pc = PB.get()
                    for c in range(8):
                        tr(pt_[:, c * 128:(c + 1) * 128], membf[:, s, c * 128:(c + 1) * 128], ident[:, :],
                           ['membf', 'ident'], [pc])
                    cp('dve', memKT[:, :, s * 128:(s + 1) * 128],
                       pt_[:, :].rearrange("p (c t) -> p c t", c=8), [pc], ['memKT'])

            def load_x(it, x_rows, TT):
                nsub = max(1, TT // 128); np_ = min(TT, 128)
                dma('sp', XTB[it % 2][0:np_, 0:nsub, :], x_rows.rearrange("(s p) d -> p s d", p=np_), [],
                    [(('XTm', it % 2), s_) for s_ in range(4)])

            def norm1_a(it, TT):
                nsub = max(1, TT // 128); np_ = min(TT, 128)
                XTm = XTB[it % 2]; xc = ('XTm', it % 2)
                mset('dve', STAT[:, 0:nsub], 0.0, [('stat', 0)])
                for s in range(nsub):
                    norm_a(XTm[0:np_, s, :], [(xc, s)], s, np_, 0)

            def norm1_b(it, TT):
                nsub = max(1, TT // 128); np_ = min(TT, 128)
                for s in range(nsub):
                    norm_b(s, np_, 0, actT, 'actT')

            def phaseC_tile(it, y_rows, TT, o_src, last_conv_out, hooks=None):
                nsub = max(1, TT // 128); np_ = min(TT, 128)
                N = TT
                XTm = XTB[it % 2]; xc = ('XTm', it % 2)
                AT = [('actT', s_) for s_ in range(nsub)]
                dma('sp', oT[:, :, 0:N], o_src, [], ['oT'])
                for half in range(2):
                    pu, cu = load_panel('u%d' % half)
                    pgc, cgc = load_panel('gc%d' % half)
                    pgb, cgb = load_panel('gb%d' % half)
                    for jj in range(4):
                        j = half * 4 + jj
                        psu, c_u = PF.get(); psc, c_c = PF.get(); psg, c_g = PF.get()
                        for (ps, pc, wp, wc) in ((psu, c_u, pu, cu), (psc, c_c, pgc, cgc), (psg, c_g, pgb, cgb)):
                            for kc in range(8):
                                mm(ps[:, 0:N], wp[:, kc, jj * 128:(jj + 1) * 128], actT[:, kc, 0:N], kc == 0, kc == 7,
                                   AT + [wc], [pc])
                        ut = TMPF[j % 2]; utc = ('tmpf', j % 2)
                        cp('act', ut[:, 0:N], psu[:, 0:N], [c_u], [utc])
                        vj = VJ[j % 2]; vjc = ('vj', j % 2)
                        vjh = ('vjh', j % 2)
                        cp('pool', vj[:, 0:2], halo[:, j, :], ['halo'], [vjh])
                        tt('dve', vj[:, 2:2 + N], ut[:, 0:N], psc[:, 0:N], ALU.mult, [utc, c_c], [vjc])
                        ct = TMPF[2 + j % 2]; ctc = ('tmpf', 2 + j % 2)
                        ts('dve', ct[:, 0:N], vj[:, 2:2 + N], wconv_c[:, 2, j:j + 1], ALU.mult, [vjc, 'wconv_c'], [ctc])
                        stt('dve', ct[:, 0:N], vj[:, 1:1 + N], wconv_c[:, 1, j:j + 1], ct[:, 0:N], ALU.mult, ALU.add,
                            [vjc, vjh, 'wconv_c', ctc], [ctc])
                        stt('dve', ct[:, 0:N], vj[:, 0:N], wconv_c[:, 0, j:j + 1], ct[:, 0:N], ALU.mult, ALU.add,
                            [vjc, vjh, 'wconv_c', ctc], [ctc])
                        cp('pool', halo[:, j, :], vj[:, N:N + 2], [vjc], ['halo'])
                        tt('dve', wT[:, j, 0:N], ct[:, 0:N], psg[:, 0:N], ALU.mult, [ctc, c_g], ['wT'])
                if last_conv_out is not None:
                    dma('pool', last_conv_out.rearrange("p (c t) -> p c t", c=8), halo[:], ['halo'], [], odd=True)
                for half in range(2):
                    pco, cco = load_panel('co%d' % half)
                    pac, cac = load_panel('ac%d' % half)
                    pmo, cmo = load_panel('mo%d' % half)
                    pam, cam = load_panel('am%d' % half)
                    for jj in range(4):
                        i = half * 4 + jj
                        psya, c_ya = PF.get(); psac, c_ac = PF.get(); psyb, c_yb = PF.get(); psam, c_am = PF.get()
                        for (ps, pc, wp, wc, src, srcc) in ((psya, c_ya, pco, cco, wT, ['wT']),
                                                            (psac, c_ac, pac, cac, actT, AT),
                                                            (psyb, c_yb, pmo, cmo, oT, ['oT']),
                                                            (psam, c_am, pam, cam, actT, AT)):
                            for kc in range(8):
                                mm(ps[:, 0:N], wp[:, kc, jj * 128:(jj + 1) * 128], src[:, kc, 0:N], kc == 0, kc == 7,
                                   srcc + [wc], [pc])
                        th1 = TMPF[0]; th2 = TMPF[1]
                        act(th1[:, 0:N], psac[:, 0:N], AF.Tanh, [c_ac], [('tmpf', 0)], scale=0.5)
                        act(th2[:, 0:N], psam[:, 0:N], AF.Tanh, [c_am], [('tmpf', 1)], scale=0.5)
                        stt('dve', th1[:, 0:N], th1[:, 0:N], 1.0, psya[:, 0:N], ALU.add, ALU.mult,
                            [('tmpf', 0), c_ya], [('tmpf', 0)])
                        stt('dve', th2[:, 0:N], th2[:, 0:N], 1.0, psyb[:, 0:N], ALU.add, ALU.mult,
                            [('tmpf', 1), c_yb], [('tmpf', 1)])
                        tt('pool', mT[:, i, 0:N], th1[:, 0:N], th2[:, 0:N], ALU.add, [('tmpf', 0), ('tmpf', 1)], ['mT'])
                def update_and_norm(srcT, src_cells, pname, half_scale, gidx, scol):
                    pans = [load_panel('%s%d' % (pname, half)) for half in range(2)]
                    mset('dve', STAT[:, scol:scol + nsub], 0.0, [('stat', scol)])
                    for s in range(nsub):
                        for half, (pw, cw) in enumerate(pans):
                            ps, pc = PF.get()
                            for kc in range(8):
                                mm(ps[0:np_, :], srcT[:, kc, s * 128:s * 128 + np_], pw[:, kc, :], kc == 0, kc == 7,
                                   src_cells + [cw], [pc])
                            xs_ = XTm[0:np_, s, half * 512:(half + 1) * 512]
                            if half_scale is None:
                                tt('dve', xs_, ps[0:np_, :], xs_, ALU.add, [pc, (xc, s)], [(xc, s)])
                            else:
                                stt('dve', xs_, ps[0:np_, :], half_scale, xs_, ALU.mult, ALU.add,
                                    [pc, (xc, s)], [(xc, s)])
                        norm_a(XTm[0:np_, s, :], [(xc, s)], s, np_, scol)
                        if s >= 2:
                            norm_b(s - 2, np_, gidx, actT, 'actT')
                    for s_ in range(max(0, nsub - 2), nsub):
                        norm_b(s_, np_, gidx, actT, 'actT')

                if hooks is not None:
                    hooks['load_next']()
                update_and_norm(mT, ['mT'], 'mx', 0.5, 1, 12)
                def fm_group(pan, pcell, split, evac):
                    banks = [PF.get() for _ in range(4)]
                    if split and nsub == 4:
                        c_sp = 3 * 128
                        for jj in range(4):
                            ps, pc = banks[jj]
                            for kc in range(8):
                                mm(ps[:, 0:c_sp], pan[:, kc, jj * 128:(jj + 1) * 128], actT[:, kc, 0:c_sp],
                                   kc == 0, kc == 7, AT[0:3] + [pcell], [pc])
                        for jj in range(4):
                            ps, pc = banks[jj]
                            for kc in range(8):
                                mm(ps[:, c_sp:N], pan[:, kc, jj * 128:(jj + 1) * 128], actT[:, kc, c_sp:N],
                                   kc == 0, kc == 7, AT[3:4] + [pcell], [pc])
                            evac(jj, ps, pc)
                    else:
                        for jj in range(4):
                            ps, pc = banks[jj]
                            for kc in range(8):
                                mm(ps[:, 0:N], pan[:, kc, jj * 128:(jj + 1) * 128], actT[:, kc, 0:N], kc == 0, kc == 7,
                                   AT + [pcell], [pc])
                            evac(jj, ps, pc)

                for half in range(2):
                    pq, cq_ = load_panel('qm%d' % half)

                    def ev_q(jj, ps, pc, half=half):
                        i = half * 4 + jj
                        cp('act', qmT[:, i, 0:N], ps[:, 0:N], [pc], [('qmT', i)])
                    fm_group(pq, cq_, half == 0, ev_q)
                def memS(hm):
                    ptm = PTm[hm % 2]; ptc = ('ptm', hm % 2)
                    for kt in range(2):
                        ps, pc = PF.get()
                        for dc in range(2):
                            mm(ps[:, 0:N], memKT[:, hm * 2 + dc, kt * 128:(kt + 1) * 128], qmT[:, hm * 2 + dc, 0:N],
                               dc == 0, dc == 1, ['memKT', ('qmT', hm * 2 + dc)], [pc])
                        act(ptm[:, kt, 0:N], ps[:, 0:N], AF.Exp, [pc], [ptc], scale=MEM_SCALE)

                memS(0)
                for hm in range(4):
                    if hm + 1 < 4:
                        memS(hm + 1)
                    ptm = PTm[hm % 2]; ptc = ('ptm', hm % 2)
                    pss, pcs = PF.get()
                    for kt in range(2):
                        mm(pss[:, 0:N], ones[:, :], ptm[:, kt, 0:N], kt == 0, kt == 1, ['ones', ptc], [pcs])
                    rcb = TMPF[hm % 2]; rcc = ('tmpf', hm % 2)
                    recip(rcb[:, 0:N], pss[:, 0:N], [pcs], [rcc])
                    for dc in range(2):
                        ps, pc = PF.get()
                        for kt in range(2):
                            mm(ps[:, 0:N], memV[:, kt, hm * 256 + dc * 128:hm * 256 + (dc + 1) * 128], ptm[:, kt, 0:N],
                               kt == 0, kt == 1, ['memV', ptc], [pc])
                        tt('dve', omT[:, hm * 2 + dc, 0:N], ps[:, 0:N], rcb[:, 0:N], ALU.mult, [pc, rcc], ['omT'])
                update_and_norm(omT, ['omT'], 'om', None, 2, 24)
                if hooks is not None:
                    hooks['norm_a']()
                for jp in range(8):
                    pup, cup = load_panel('up%d' % jp)

                    def ev_u(jj, ps, pc, jp=jp):
                        c = jp * 4 + jj
                        rt = TMPF[2 + c % 2]; rtc = ('tmpf', 2 + c % 2)
                        act(rt[:, 0:N], ps[:, 0:N], AF.Relu, [pc], [rtc])
                        tt('pool', hid[:, c, 0:N], rt[:, 0:N], rt[:, 0:N], ALU.mult, [rtc], [('hid', c)])
                    fm_group(pup, cup, jp == 0, ev_u)
                if hooks is not None:
                    hooks['norm_b']()
                for half in range(2):
                    accs = [PF.get() for _ in range(nsub)]
                    for g in range(4):
                        pdn, cdn = load_panel('dn%d_%d' % (g, half))
                        for s in range(nsub):
                            ps, pc = accs[s]
                            for kc in range(8):
                                kk = g * 8 + kc
                                mm(ps[0:np_, :], hid[:, kk, s * 128:s * 128 + np_], pdn[:, kc, :], kk == 0, kk == 31,
                                   [('hid', kk), cdn], [pc])
                    for s in range(nsub):
                        ps, pc = accs[s]
                        tt('dve', XTm[0:np_, s, half * 512:(half + 1) * 512], ps[0:np_, :],
                           XTm[0:np_, s, half * 512:(half + 1) * 512], ALU.add, [pc, (xc, s)], [(xc, s)])
                stc = ('stat', 36)
                mset('dve', STAT[:, 36:40], 0.0, [stc])
                for s in range(nsub):
                    act(junk[0:np_, :], XTm[0:np_, s, :], AF.Square, [(xc, s), stc], ['junkC', stc],
                        accum=STAT[0:np_, 36 + s:37 + s])
                    rstd_ops(STAT[0:np_, 36 + s:37 + s], STAT[0:np_, 40 + s:41 + s], STAT[0:np_, 44 + s:45 + s],
                             float(D), stc)
                    stt('dve', XTm[0:np_, s, :], XTm[0:np_, s, :], STAT[0:np_, 44 + s:45 + s], gfin_b[0:np_, :],
                        ALU.mult, ALU.mult, [(xc, s), stc, 'gfin_b'], [(xc, s)])
                dma('pool', y_rows.rearrange("(s p) d -> p s d", p=np_), XTm[0:np_, 0:nsub, :],
                    [(xc, s_) for s_ in range(nsub)], [])

            xs_list = [(x_s, TS)] + [(x_p[ti * 512:(ti + 1) * 512, :], 512) for ti in range(8)]
            load_x(0, *xs_list[0])
            prep_mem_sample()
            dma('sp', halo[:], cconv_h.rearrange("p (c t) -> p c t", c=8), [], ['halo'], odd=True)
            load_x(1, *xs_list[1])
            norm1_a(0, TS)
            norm1_b(0, TS)
            phaseC_tile(0, y_s, TS, osc_s.rearrange("c p t -> p c t"), nconv_s)
            prep_mem_prompt()
            mset('dve', halo[:], 0.0, ['halo'])
            norm1_a(1, 512)
            norm1_b(1, 512)
            nop = lambda: None
            for ti in range(8):
                j = ti + 2
                if j < len(xs_list):
                    hk = dict(load_next=(lambda j_: (lambda: load_x(j_, *xs_list[j_])))(j),
                              norm_a=(lambda j_: (lambda: norm1_a(j_, 512)))(j),
                              norm_b=(lambda j_: (lambda: norm1_b(j_, 512)))(j))
                else:
                    hk = dict(load_next=nop, norm_a=nop, norm_b=nop)
                if ti == 0:
                    pass
                phaseC_tile(ti + 1, y_p[ti * 512:(ti + 1) * 512, :], 512,
                            osc[:, :, ti * 512:(ti + 1) * 512].rearrange("c p t -> p c t"),
                            nconv_p if ti == 7 else None, hk)
            if 'C' in phases:
                P.run(wait_carry_on='sp', final=True)
            else:
                P._reset()
    return nc


_CACHE = {}


def _rope_tables():
    half = 16
    inv = (np.float32(10000.0) ** (-np.arange(half, dtype=np.float32) / np.float32(half))).astype(np.float32)
    pos = np.concatenate([np.arange(T, dtype=np.float32), PAST + np.arange(TS, dtype=np.float32)])
    ang = (pos[:, None] * inv[None, :]).astype(np.float32)
    cos = np.cos(ang).astype(np.float32)
    sin = np.sin(ang).astype(np.float32)
    rt_tm = np.concatenate([cos, cos, sin, sin], axis=1).astype(np.float32)
    rt_c = np.ascontiguousarray(np.tile(np.concatenate([cos, cos], axis=1).T, (4, 1)))
    rt_s = np.ascontiguousarray(np.tile(np.concatenate([-sin, sin], axis=1).T, (4, 1)))
    return rt_tm, rt_c, rt_s


def kernel(x_prompt, x_sample, cache_conv, cache_ckv, cache_krope, cache_mem_k, cache_mem_v, mem_prompt,
           g_mix, w_in, w_conv, w_conv_out, g_q, w_uq, g_kv, w_ukv, w_mla_out, w_mix_out,
           g_mem_q, g_mem_kv, w_qm, w_km, w_vm, w_om, g_mlp, w_up, w_down, g_final):
    if 'nc' not in _CACHE:
        _CACHE['nc'] = build_program()
    nc = _CACHE['nc']
    in_maps = _prep(x_prompt, x_sample, cache_conv, cache_ckv, cache_krope, cache_mem_k, cache_mem_v, mem_prompt,
                    g_mix, w_in, w_conv, w_conv_out, g_q, w_uq, g_kv, w_ukv, w_mla_out, w_mix_out,
                    g_mem_q, g_mem_kv, w_qm, w_km, w_vm, w_om, g_mlp, w_up, w_down, g_final)
    res = run_bass_kernel_spmd(nc, in_maps, core_ids=list(range(N_CORES)))
    return _gather(res.results)


def _prep(x_prompt, x_sample, cache_conv, cache_ckv, cache_krope, cache_mem_k, cache_mem_v, mem_prompt,
          g_mix, w_in, w_conv, w_conv_out, g_q, w_uq, g_kv, w_ukv, w_mla_out, w_mix_out,
          g_mem_q, g_mem_kv, w_qm, w_km, w_vm, w_om, g_mlp, w_up, w_down, g_final):
    f = lambda a: np.ascontiguousarray(np.asarray(a, dtype=np.float32))
    rt_tm, rt_c, rt_s = _rope_tables()
    w_uq0 = f(w_uq[0])
    idx = np.concatenate([np.concatenate([np.arange(h * 96 + 80, h * 96 + 96), np.arange(h * 96 + 64, h * 96 + 80)])
                          for h in range(NH)])
    w_uqs = np.ascontiguousarray(w_uq0[:, idx])
    idx_n = np.concatenate([np.arange(h * 96, h * 96 + 64) for h in range(NH)])
    idx_r = np.concatenate([np.arange(h * 96 + 64, h * 96 + 96) for h in range(NH)])
    w_uqn = np.ascontiguousarray(w_uq0[:, idx_n])
    w_uqr = np.ascontiguousarray(w_uq0[:, idx_r])
    shared = dict(
        g_mix=f(g_mix[0]), w_in=f(w_in[0]), w_conv=f(w_conv[0]), w_conv_out=f(w_conv_out[0]), g_q=f(g_q[0]),
        w_uqn=w_uqn, w_uqr=w_uqr, w_uqs=w_uqs, g_kv=f(g_kv[0]), w_ukv=f(w_ukv[0]), w_mla_out=f(w_mla_out[0]),
        w_mix_out=f(w_mix_out[0]), g_mem_q=f(g_mem_q[0]), g_mem_kv=f(g_mem_kv[0]), w_qm=f(w_qm[0]),
        w_km=f(w_km[0]), w_vm=f(w_vm[0]), w_om=f(w_om[0]), g_mlp=f(g_mlp[0]), w_up=f(w_up[0]),
        w_down=f(w_down[0]), g_final=f(g_final), identf=np.eye(128, dtype=np.float32),
        rt_tm=rt_tm, rt_c=rt_c, rt_s=rt_s,
        gcols_h=np.ascontiguousarray(np.stack([f(g_mix[0]), f(g_mem_q[0]), f(g_mlp[0]), f(g_mem_kv[0])])
                                     .reshape(4, 8, 128).transpose(2, 0, 1).reshape(128, 32)),
        wconv_h=np.ascontiguousarray(f(w_conv[0]).reshape(3, 8, 128).transpose(2, 0, 1).reshape(128, 24)))
    in_maps = []
    for b in range(N_CORES):
        m = dict(shared)
        m.update(x_p=f(x_prompt[b]), x_s=f(x_sample[b]), cconv=f(cache_conv[0, b]), cckv=f(cache_ckv[0, b]),
                 ckr=f(cache_krope[0, b]),
                 cconv_h=np.ascontiguousarray(f(cache_conv[0, b]).reshape(2, 8, 128).transpose(2, 1, 0).reshape(128, 16)), cmk=f(np.asarray(cache_mem_k[0, b]).reshape(NMEM, D)),
                 cmv=f(np.asarray(cache_mem_v[0, b]).reshape(NMEM, D)), memp=f(mem_prompt[b]))
        in_maps.append(m)
    return in_maps


def _gather(R):
    st = lambda k: np.stack([np.asarray(R[b][k], dtype=np.float32) for b in range(len(R))])
    cv = lambda k: np.ascontiguousarray(st(k).reshape(len(R), 128, 8, 2).transpose(0, 3, 2, 1).reshape(len(R), 2, D))
    y_prompt = st('y_p')
    y_sample = st('y_s')
    return (y_prompt, y_sample,
            cv('nconv_p')[None], st('nckv_p')[None], st('nkr_p')[None],
            st('nmk_p').reshape(1, N_CORES, NMEM, 4, 256), st('nmv_p').reshape(1, N_CORES, NMEM, 4, 256),
            cv('nconv_s')[None], st('nckv_s')[None], st('nkr_s')[None])
```

```python
import numpy as np
from contextlib import ExitStack
import concourse.bass as bass
import concourse.mybir as mybir
from concourse.bass_utils import run_bass_kernel_spmd

F32 = mybir.dt.float32
BF16 = mybir.dt.bfloat16
ALU = mybir.AluOpType
AF = mybir.ActivationFunctionType

ENGS = ['pe', 'act', 'dve', 'pool', 'sp']

D = 1024
T = 4096
TS = 64
PAST = 1024
NH = 16
DFF = 4096
NMEM = 256
EPS = 1e-6
ATTN_SCALE = 96.0 ** -0.5
MEM_SCALE = 256.0 ** -0.5
IN_COLS = 5920
N_CORES = 8
DMA_RING = 8


class _Op:
    __slots__ = ('eng', 'fn', 'deps', 'is_dma', 'signal', 'sem', 'val', 'dma_j', 'carry', 'odd')


class Prog:
    def __init__(self, nc, es, dma_ring=8):
        self.nc = nc
        self.ring = dma_ring
        self.es = es
        self.nodd = 0
        self.esem = {e: es.enter_context(nc.semaphore('s_' + e)) for e in ENGS}
        self.dsem = {e: [es.enter_context(nc.semaphore('d_%s_%d' % (e, i))) for i in range(dma_ring)]
                     for e in ('sp', 'pool', 'act')}
        self.rank = {e: 0 for e in ENGS}
        self.ndma = {e: 0 for e in ENGS}
        self.dma_hist = {e: [] for e in ENGS}
        self.carry = []
        self._reset()

    def _reset(self):
        self.ops = {e: [] for e in ENGS}
        self.cells = {}

    def emit(self, eng, fn, reads=(), writes=(), dma=False, carry=False, odd=False):
        op = _Op()
        op.odd = odd
        op.eng = eng
        op.fn = fn
        op.is_dma = dma
        op.signal = dma
        op.sem = None
        op.val = 0
        op.dma_j = -1
        op.carry = carry
        deps = {}
        for c in reads:
            st = self.cells.get(c)
            if st is not None and st[0] is not None:
                deps[id(st[0])] = (st[0], True)
            if st is not None and eng != 'pe' and isinstance(c, str) and c.startswith('ps'):
                for r in st[1]:
                    if r.eng != eng and id(r) not in deps:
                        deps[id(r)] = (r, False)
        for c in writes:
            st = self.cells.get(c)
            if st is None:
                continue
            if st[0] is not None and id(st[0]) not in deps:
                deps[id(st[0])] = (st[0], False)
            for r in st[1]:
                if id(r) not in deps:
                    deps[id(r)] = (r, False)
        real = []
        for d, raw in deps.values():
            if d.eng == eng and not d.is_dma and not dma and eng == 'pe':
                continue
            real.append(d)
            d.signal = True
        op.deps = real
        for c in reads:
            st = self.cells.setdefault(c, [None, []])
            if not dma:
                st[1] = [r for r in st[1] if r.is_dma or r.eng != eng]
            st[1].append(op)
        for c in writes:
            self.cells[c] = [op, []]
        self.ops[eng].append(op)
        return op

    def run(self, wait_carry_on=None, final=False):
        nc = self.nc
        last_tok = {}
        for e in ENGS:
            comp = [op for op in self.ops[e] if not op.is_dma]
            if comp:
                comp[-1].signal = True
        for e in ENGS:
            for op in self.ops[e]:
                if op.is_dma and op.odd:
                    op.sem = self.es.enter_context(nc.semaphore('o_%d' % self.nodd))
                    self.nodd += 1
                    op.val = 16
                elif op.is_dma:
                    op.dma_j = self.ndma[e]
                    self.ndma[e] += 1
                    op.sem = self.dsem[e][op.dma_j % self.ring]
                    op.val = 16 * (op.dma_j // self.ring + 1)
                    self.dma_hist[e].append((op.sem, op.val))
                elif op.signal:
                    self.rank[e] += 1
                    op.sem = self.esem[e]
                    op.val = self.rank[e]
            comp = [op for op in self.ops[e] if not op.is_dma]
            if comp:
                last_tok[e] = (comp[-1].sem, comp[-1].val)
        bar_dma = {}
        new_carry = []
        for e in ENGS:
            for op in self.ops[e]:
                if op.is_dma:
                    if op.carry:
                        new_carry.append((op.sem, op.val))
                    else:
                        k = id(op.sem)
                        if k not in bar_dma or bar_dma[k][1] < op.val:
                            bar_dma[k] = (op.sem, op.val)
        old_carry = list(self.carry)
        if wait_carry_on is not None:
            self.carry = []
        self.carry += new_carry
        if final:
            for sem, val in self.carry:
                k = id(sem)
                if k not in bar_dma or bar_dma[k][1] < val:
                    bar_dma[k] = (sem, val)
        ring = self.ring
        with nc.Block() as block:
            hooks = {'pe': block.tensor, 'act': block.scalar, 'dve': block.vector,
                     'pool': block.gpsimd, 'sp': block.sync}

            def make(e):
                def body(eng):
                    waited = {}

                    def wait(sem, val):
                        k = id(sem)
                        if waited.get(k, 0) >= val:
                            return
                        waited[k] = val
                        eng.wait_ge(sem, val)

                    if wait_carry_on == e:
                        for sem, val in old_carry:
                            wait(sem, val)
                    for op in self.ops[e]:
                        if op.is_dma and not op.odd and op.dma_j >= ring:
                            ps, pv = self.dma_hist[e][op.dma_j - ring]
                            wait(ps, pv)
                        for d in op.deps:
                            wait(d.sem, d.val)
                        ins = op.fn(eng)
                        if op.is_dma:
                            ins.then_inc(op.sem, 16)
                        elif op.signal:
                            ins.then_inc(op.sem, 1)
                    for f in ENGS:
                        if f != e and f in last_tok:
                            wait(*last_tok[f])
                    for sem, val in bar_dma.values():
                        wait(sem, val)
                return body

            for e in ENGS:
                hooks[e](make(e))
        self._reset()


class RR:
    def __init__(self, items):
        self.items = items
        self.i = 0

    def get(self):
        it = self.items[self.i % len(self.items)]
        self.i += 1
        return it


def build_program(phases='ABC', conv=True, dbg_stage=9, dbg_tiles=99):
    nc = bass.Bass("TRN2", target_bir_lowering=False)

    def din(name, shape):
        return nc.dram_tensor(name, list(shape), F32, kind="ExternalInput").ap()

    def dout(name, shape):
        return nc.dram_tensor(name, list(shape), F32, kind="ExternalOutput").ap()

    x_p = din("x_p", [T, D]); x_s = din("x_s", [TS, D])
    cconv = din("cconv", [2, D]); cckv = din("cckv", [PAST, 256]); ckr = din("ckr", [PAST, 32])
    cmk = din("cmk", [NMEM, D]); cmv = din("cmv", [NMEM, D]); memp = din("memp", [NMEM, D])
    g_mix = din("g_mix", [D]); w_in = din("w_in", [D, IN_COLS]); w_conv = din("w_conv", [3, D])
    w_conv_out = din("w_conv_out", [D, D]); g_q = din("g_q", [512]); w_uqn = din("w_uqn", [512, 1024])
    w_uqr = din("w_uqr", [512, 512])
    w_uqs = din("w_uqs", [512, 512]); g_kv = din("g_kv", [256]); w_ukv = din("w_ukv", [256, 2048])
    w_mla_out = din("w_mla_out", [D, D]); w_mix_out = din("w_mix_out", [D, D])
    g_mem_q = din("g_mem_q", [D]); g_mem_kv = din("g_mem_kv", [D])
    w_qm = din("w_qm", [D, D]); w_km = din("w_km", [D, D]); w_vm = din("w_vm", [D, D]); w_om = din("w_om", [D, D])
    g_mlp = din("g_mlp", [D]); w_up = din("w_up", [D, DFF]); w_down = din("w_down", [DFF, D]); g_final = din("g_final", [D])
    identf = din("identf", [128, 128])
    gcols_h = din("gcols_h", [128, 32])
    wconv_h = din("wconv_h", [128, 24])
    cconv_h = din("cconv_h", [128, 16])
    rt_tm = din("rt_tm", [T + TS, 64]); rt_c = din("rt_c", [128, T + TS]); rt_s = din("rt_s", [128, T + TS])

    y_p = dout("y_p", [T, D]); y_s = dout("y_s", [TS, D])
    nconv_p = dout("nconv_p", [128, 16]); nckv_p = dout("nckv_p", [T, 256]); nkr_p = dout("nkr_p", [T, 32])
    nmk_p = dout("nmk_p", [NMEM, D]); nmv_p = dout("nmv_p", [NMEM, D])
    nconv_s = dout("nconv_s", [128, 16]); nckv_s = dout("nckv_s", [TS, 256]); nkr_s = dout("nkr_s", [TS, 32])

    pan_spec = {}
    def addpan(name, src, r0, c0):
        pan_spec[name] = (len(pan_spec), src, r0, c0)
    for i in range(2):
        addpan('u%d' % i, w_in, 0, 0 + 512 * i)
        addpan('gb%d' % i, w_in, 0, 1024 + 512 * i)
        addpan('gc%d' % i, w_in, 0, 2048 + 512 * i)
        addpan('ac%d' % i, w_in, 0, 3872 + 512 * i)
        addpan('am%d' % i, w_in, 0, 4896 + 512 * i)
        addpan('co%d' % i, w_conv_out, 0, 512 * i)
        addpan('mo%d' % i, w_mla_out, 0, 512 * i)
        addpan('mx%d' % i, w_mix_out, 0, 512 * i)
        addpan('qm%d' % i, w_qm, 0, 512 * i)
        addpan('om%d' % i, w_om, 0, 512 * i)
        addpan('km%d' % i, w_km, 0, 512 * i)
        addpan('vm%d' % i, w_vm, 0, 512 * i)
    for j in range(8):
        addpan('up%d' % j, w_up, 0, 512 * j)
    for g in range(4):
        for h in range(2):
            addpan('dn%d_%d' % (g, h), w_down, 1024 * g, 512 * h)
    NPAN = len(pan_spec)
    wsc = nc.dram_tensor("wsc", [NPAN, 128, 4096], BF16, kind="Internal").ap()
    qscn = nc.dram_tensor("qscn", [8, 128, T], BF16, kind="Internal").ap()
    qscr = nc.dram_tensor("qscr", [4, 128, T], BF16, kind="Internal").ap()
    qscn_s = nc.dram_tensor("qscn_s", [8, 128, TS], BF16, kind="Internal").ap()
    qscr_s = nc.dram_tensor("qscr_s", [4, 128, TS], BF16, kind="Internal").ap()
    osc = nc.dram_tensor("osc", [8, 128, T], BF16, kind="Internal").ap()
    osc_s = nc.dram_tensor("osc_s", [8, 128, TS], BF16, kind="Internal").ap()

    with ExitStack() as gs:
        P = Prog(nc, gs, dma_ring=DMA_RING)

        def dma(q, out, in_, reads, writes, carry=False, odd=False):
            P.emit(q, lambda e: e.dma_start(out=out, in_=in_), reads, writes, dma=True, carry=carry, odd=odd)

        def mm(out, lhsT, rhs, start, stop, reads, writes):
            P.emit('pe', lambda e: e.matmul(out, lhsT=lhsT, rhs=rhs, start=start, stop=stop), reads, writes)

        def tr(out, in_, ident, reads, writes):
            P.emit('pe', lambda e: e.transpose(out, in_, ident), reads, writes)

        def act(out, in_, func, reads, writes, scale=1.0, bias=0.0, accum=None):
            if accum is None:
                P.emit('act', lambda e: e.activation(out=out, in_=in_, func=func, bias=bias, scale=scale), reads, writes)
            else:
                P.emit('act', lambda e: e.activation(out=out, in_=in_, func=func, bias=bias, scale=scale,
                                                     accum_out=accum), reads, writes)

        def tt(eng, out, in0, in1, op, reads, writes):
            P.emit(eng, lambda e: e.tensor_tensor(out=out, in0=in0, in1=in1, op=op), reads, writes)

        def ts(eng, out, in0, s1, op0, reads, writes, s2=None, op1=None):
            if op1 is None:
                P.emit(eng, lambda e: e.tensor_scalar(out=out, in0=in0, scalar1=s1, scalar2=None, op0=op0), reads, writes)
            else:
                P.emit(eng, lambda e: e.tensor_scalar(out=out, in0=in0, scalar1=s1, scalar2=s2, op0=op0, op1=op1),
                       reads, writes)

        def stt(eng, out, in0, scalar, in1, op0, op1, reads, writes):
            P.emit(eng, lambda e: e.scalar_tensor_tensor(out=out, in0=in0, scalar=scalar, in1=in1, op0=op0, op1=op1),
                   reads, writes)

        def cp(eng, out, in_, reads, writes):
            if eng == 'act':
                P.emit('act', lambda e: e.copy(out=out, in_=in_), reads, writes)
            else:
                P.emit(eng, lambda e: e.tensor_copy(out=out, in_=in_), reads, writes)

        def mset(eng, ap, val, writes):
            P.emit(eng, lambda e: e.memset(ap, val), (), writes)

        def recip(out, in_, reads, writes):
            P.emit('dve', lambda e: e.reciprocal(out=out, in_=in_), reads, writes)

        def rstd_ops(ssq_ap, lnt_ap, rstd_ap, n, cell):
            act(lnt_ap, ssq_ap, AF.Ln, [cell], [cell], scale=1.0 / n, bias=EPS)
            act(rstd_ap, lnt_ap, AF.Exp, [cell], [cell], scale=-0.5)

        with ExitStack() as sAB:
            def sb(name, shape, dt):
                return sAB.enter_context(nc.sbuf_tensor(name, list(shape), dt))
            ident = sb("ident", [128, 128], BF16)
            w_insub = sb("w_insub", [128, 8, 800], BF16)
            w_uqn_sb = sb("w_uqn_sb", [128, 4, 1024], BF16)
            w_uqr_sb = sb("w_uqr_sb", [128, 4, 512], BF16)
            w_uqs_sb = sb("w_uqs_sb", [128, 4, 512], BF16)
            w_ukv_sb = sb("w_ukv_sb", [128, 2, 2048], BF16)
            gq_b = sb("gq_b", [128, 512], F32)
            gkv_b = sb("gkv_b", [128, 256], F32)
            gmix_c = sb("gmix_c", [128, 8], F32)
            ckvT_p = sb("ckvT_p", [128, 2, T], BF16)
            ckvT_s = sb("ckvT_s", [128, 2, PAST + TS], BF16)
            Kb = [sb("Kb%d" % i, [96, T], BF16) for i in range(2)]
            KsB = [sb("Ks%d" % i, [96, PAST + TS], BF16) for i in range(2)]
            Ks = KsB[0]

            dma('pool', ident[:], identf, [], ['ident'])
            dma('pool', w_insub[:, 0:4, :], w_in[0:512, 3072:3872].rearrange("(kc p) c -> p kc c", p=128), [], [('w_insub', 0)])
            dma('pool', w_insub[:, 4:8, :], w_in[512:1024, 3072:3872].rearrange("(kc p) c -> p kc c", p=128), [], [('w_insub', 1)])
            dma('pool', w_uqn_sb[:], w_uqn.rearrange("(kc p) c -> p kc c", p=128), [], ['w_uq'])
            dma('pool', w_uqr_sb[:], w_uqr.rearrange("(kc p) c -> p kc c", p=128), [], ['w_uq'])
            dma('pool', w_uqs_sb[:], w_uqs.rearrange("(kc p) c -> p kc c", p=128), [], ['w_uqs'])
            dma('pool', w_ukv_sb[:], w_ukv.rearrange("(kc p) c -> p kc c", p=128), [], ['w_ukv'])
            dma('sp', gq_b[:], g_q.partition_broadcast(128), [], ['gq_b'], odd=True)
            dma('sp', gkv_b[:], g_kv.partition_broadcast(128), [], ['gkv_b'], odd=True)
            dma('sp', gmix_c[:], gcols_h[:, 0:8], [], ['gmix_c'], odd=True)

            with ExitStack() as sA:
                def sa(name, shape, dt):
                    return sA.enter_context(nc.sbuf_tensor(name, list(shape), dt))
                XT = [sa("xt%d" % i, [128, 4, D], F32) for i in range(2)]
                XN = [sa("xn%d" % i, [128, D], BF16) for i in range(4)]
                junk = sa("junkA", [128, D], BF16)
                nT = sa("nT", [128, 8, 512], BF16)
                CQN = [sa("cqn%d" % i, [128, 512], BF16) for i in range(2)]
                CQNT = [sa("cqnT%d" % i, [128, 4, 512], BF16) for i in range(2)]
                CKVO = [sa("ckvo%d" % i, [128, 4, 256], F32) for i in range(2)]
                CKVB = [sa("ckvb%d" % i, [128, 256], BF16) for i in range(2)]
                KRO = [sa("kro%d" % i, [128, 4, 32], F32) for i in range(2)]
                KRP = [sa("krp%d" % i, [128, 96], BF16) for i in range(2)]
                RTM = [sa("rtm%d" % i, [128, 4, 64], F32) for i in range(2)]
                RFC = [sa("rfc%d" % i, [128, 512], F32) for i in range(2)]
                RFS = [sa("rfs%d" % i, [128, 512], F32) for i in range(2)]
                QN = [sa("qn%d" % i, [128, 512], BF16) for i in range(2)]
                QR = [sa("qr%d" % i, [128, 512], BF16) for i in range(2)]
                T1 = [sa("t1_%d" % i, [128, 512], F32) for i in range(2)]
                T2 = [sa("t2_%d" % i, [128, 512], F32) for i in range(2)]
                TA = [sa("ta%d" % i, [128, 32], F32) for i in range(2)]
                TB = [sa("tb%d" % i, [128, 32], F32) for i in range(2)]
                STAT = [sa("stat%d" % i, [128, 36], F32) for i in range(2)]
                cckv_bf = sa("cckv_bf", [128, 8, 256], BF16)
                ckr_pad = sa("ckr_pad", [128, 8, 96], BF16)
                psf = [sA.enter_context(nc.psum_tensor("psfA%d" % i, [128, 512], F32)) for i in range(6)]
                psb = [sA.enter_context(nc.psum_tensor("psbA%d" % i, [128, 1024], BF16)) for i in range(2)]
                PF = RR([(psf[i], 'psf%d' % i) for i in range(6)])
                PB = RR([(psb[i], 'psb%d' % i) for i in range(2)])

                dma('pool', cckv_bf[:], cckv.rearrange("(t p) c -> p t c", p=128), [], ['cckv_bf'])
                mset('dve', ckr_pad[:], 0.0, ['ckr_pad'])
                mset('dve', KRP[0][:], 0.0, ['krp0'])
                mset('dve', KRP[1][:], 0.0, ['krp1'])
                dma('pool', ckr_pad[:, :, 64:96], ckr.rearrange("(t p) c -> p t c", p=128), [], ['ckr_pad'], odd=True)

                evi = [0]
                def evac_copy(out, in_, reads, writes):
                    eng = 'act' if evi[0] % 2 == 0 else 'dve'
                    evi[0] += 1
                    cp(eng, out, in_, reads, writes)

                def sample_past():
                    for t in range(8):
                        pt_, pc = PB.get()
                        for c in range(2):
                            tr(pt_[:, c * 128:(c + 1) * 128], cckv_bf[:, t, c * 128:(c + 1) * 128], ident[:, :],
                               ['cckv_bf', 'ident'], [pc])
                        evac_copy(ckvT_s[:, :, t * 128:(t + 1) * 128],
                                  pt_[:, 0:256].rearrange("p (c t) -> p c t", c=2), [pc], [('ckvT_s', t)])
                        pt2, pc2 = PB.get()
                        tr(pt2[0:96, 0:128], ckr_pad[:, t, :], ident[:, :], ['ckr_pad', 'ident'], [pc2])
                        evac_copy(KsB[0][64:96, t * 128:(t + 1) * 128], pt2[64:96, 0:128], [pc2], [('Ks0_r', t)])
                        evac_copy(KsB[1][64:96, t * 128:(t + 1) * 128], pt2[64:96, 0:128], [pc2], [('Ks1_r', t)])

                def load_tile(it, x_rows, TT, rt0):
                    nsub = max(1, TT // 128); np_ = min(TT, 128)
                    b = it % 2
                    dma('sp', XT[b][0:np_, 0:nsub, :], x_rows.rearrange("(s p) d -> p s d", p=np_), [], [('xt', b)])
                    dma('sp', RTM[b][0:np_, 0:nsub, :], rt_tm[rt0:rt0 + TT, :].rearrange("(s p) c -> p s c", p=np_),
                        [], [('rtm', b)])

                def load_rf(it, TT, rt0):
                    b = it % 2
                    dma('sp', RFC[b][:, 0:TT], rt_c[:, rt0:rt0 + TT], [], [('rfc', b)])
                    dma('sp', RFS[b][:, 0:TT], rt_s[:, rt0:rt0 + TT], [], [('rfs', b)])

                def nrm_a(it, TT, s):
                    np_ = min(TT, 128); b = it % 2
                    xt = XT[b]; st = STAT[b]; stc = ('stat', b)
                    act(junk[0:np_, :], xt[0:np_, s, :], AF.Square, [('xt', b), stc], ['junkA', stc],
                        accum=st[0:np_, s:s + 1])
                    rstd_ops(st[0:np_, s:s + 1], st[0:np_, 12 + s:13 + s], st[0:np_, 24 + s:25 + s], float(D), stc)
                    ts('dve', XN[s % 4][0:np_, :], xt[0:np_, s, :], st[0:np_, 24 + s:25 + s], ALU.mult,
                       [('xt', b), stc], [('xn', s % 4)])

                def nrm_b(TT, s):
                    np_ = min(TT, 128)
                    xn = XN[s % 4]; xnc = ('xn', s % 4)
                    pt_, pc = PB.get()
                    for c in range(8):
                        tr(pt_[:, c * 128:c * 128 + np_], xn[0:np_, c * 128:(c + 1) * 128], ident[0:np_, 0:np_],
                           [xnc, 'ident'], [pc])
                    tt('dve', nT[:, :, s * 128:s * 128 + np_],
                       pt_[:, :].rearrange("p (c t) -> p c t", c=8)[:, :, 0:np_],
                       gmix_c[:, 0:8].unsqueeze(2).broadcast_to([128, 8, np_]), ALU.mult,
                       [pc, 'gmix_c'], [('nT', s)])

                def norm_full(it, TT):
                    nsub = max(1, TT // 128)
                    mset('dve', STAT[it % 2][:, 0:12], 0.0, [('stat', it % 2)])
                    for s in range(nsub):
                        nrm_a(it, TT, s)
                    for s in range(nsub):
                        nrm_b(TT, s)

                def phaseA_tile(it, TT, ckvT_dst, ckvT_name, ckv_col0, kr_dsts, out_ckv, out_kr, qdst, pending_q,
                                nxt):
                    nsub = max(1, TT // 128); np_ = min(TT, 128)
                    b = it % 2
                    st = STAT[b]; stc = ('stat', b)
                    cqnT = CQNT[b]
                    if nxt is not None:
                        mset('dve', STAT[nxt[0] % 2][:, 0:12], 0.0, [('stat', nxt[0] % 2)])
                        nsub_n = max(1, nxt[1] // 128)
                        nxt_a = [(lambda s_: (lambda: nrm_a(nxt[0], nxt[1], s_)))(s_) for s_ in range(nsub_n)]
                    else:
                        nxt_a = []
                    held = {}

                    def part_a(s):
                        psA, cA = PF.get()
                        psB, cB = PF.get()
                        for kc in range(8):
                            mm(psA[0:np_, 0:512], nT[:, kc, s * 128:s * 128 + np_], w_insub[:, kc, 0:512],
                               kc == 0, kc == 7, [('nT', s), ('w_insub', 0), ('w_insub', 1)], [cA])
                        for kc in range(8):
                            mm(psB[0:np_, 0:288], nT[:, kc, s * 128:s * 128 + np_], w_insub[:, kc, 512:800],
                               kc == 0, kc == 7, [('nT', s), ('w_insub', 0), ('w_insub', 1)], [cB])
                        if nsub == 1:
                            while pending_q:
                                pending_q.pop(0)()
                        elif pending_q:
                            pending_q.pop(0)()
                        act(junk[0:np_, 0:512], psA[0:np_, 0:512], AF.Square, [cA, stc], ['junkA', stc],
                            accum=st[0:np_, 4 + s:5 + s])
                        act(junk[0:np_, 0:256], psB[0:np_, 0:256], AF.Square, [cB, stc], ['junkA', stc],
                            accum=st[0:np_, 8 + s:9 + s])
                        rstd_ops(st[0:np_, 4 + s:5 + s], st[0:np_, 16 + s:17 + s], st[0:np_, 28 + s:29 + s], 512.0, stc)
                        rstd_ops(st[0:np_, 8 + s:9 + s], st[0:np_, 20 + s:21 + s], st[0:np_, 32 + s:33 + s], 256.0, stc)
                        cqn = CQN[s % 2]; cqc = ('cqn', s % 2)
                        stt('dve', cqn[0:np_, :], psA[0:np_, 0:512], st[0:np_, 28 + s:29 + s], gq_b[0:np_, :],
                            ALU.mult, ALU.mult, [cA, stc, 'gq_b'], [cqc])
                        ckvo = CKVO[b]
                        stt('dve', ckvo[0:np_, s, :], psB[0:np_, 0:256], st[0:np_, 32 + s:33 + s], gkv_b[0:np_, :],
                            ALU.mult, ALU.mult, [cB, stc, 'gkv_b'], [('ckvo', b, s)])
                        ckvb = CKVB[s % 2]; ckc = ('ckvb', s % 2)
                        cp('act', ckvb[0:np_, :], ckvo[0:np_, s, :], [('ckvo', b, s)], [ckc])
                        ta = TA[s % 2]; tb = TB[s % 2]; tac = ('ta', s % 2)
                        tt('dve', ta[0:np_, :], psB[0:np_, 256:288], RTM[b][0:np_, s, 0:32], ALU.mult,
                           [cB, ('rtm', b)], [tac])
                        tt('dve', tb[0:np_, :], psB[0:np_, 256:288], RTM[b][0:np_, s, 32:64], ALU.mult,
                           [cB, ('rtm', b)], [tac])
                        kro = KRO[b]
                        tt('dve', kro[0:np_, s, 0:16], ta[0:np_, 0:16], tb[0:np_, 16:32], ALU.subtract,
                           [tac], [('kro', b, s)])
                        tt('dve', kro[0:np_, s, 16:32], ta[0:np_, 16:32], tb[0:np_, 0:16], ALU.add,
                           [tac], [('kro', b, s)])
                        krp = KRP[s % 2]; krc = 'krp%d' % (s % 2)
                        cp('act', krp[0:np_, 64:96], kro[0:np_, s, :], [('kro', b, s)], [krc])
                        if nsub == 1:
                            while nxt_a:
                                nxt_a.pop(0)()
                        elif nxt_a:
                            nxt_a.pop(0)()

                    def part_b(s):
                        cqn = CQN[s % 2]; cqc = ('cqn', s % 2)
                        pt_, pc = PB.get()
                        for c in range(4):
                            tr(pt_[:, c * 128:c * 128 + np_], cqn[0:np_, c * 128:(c + 1) * 128], ident[0:np_, 0:np_],
                               [cqc, 'ident'], [pc])
                        ckvb = CKVB[s % 2]; ckc = ('ckvb', s % 2)
                        for c in range(2):
                            tr(pt_[:, 512 + c * 128:512 + c * 128 + np_], ckvb[0:np_, c * 128:(c + 1) * 128],
                               ident[0:np_, 0:np_], [ckc, 'ident'], [pc])
                        krp = KRP[s % 2]; krc = 'krp%d' % (s % 2)
                        tr(pt_[0:96, 768:768 + np_], krp[0:np_, :], ident[0:np_, 0:np_], [krc, 'ident'], [pc])
                        cp('act', cqnT[:, :, s * 128:s * 128 + np_],
                           pt_[:, 0:512].rearrange("p (c t) -> p c t", c=4)[:, :, 0:np_], [pc], [('cqnT', b, s)])
                        col = ckv_col0 + s * 128
                        cp('dve', ckvT_dst[:, :, col:col + np_],
                           pt_[:, 512:768].rearrange("p (c t) -> p c t", c=2)[:, :, 0:np_], [pc],
                           [(ckvT_name, col // 128)])
                        for (dst, dname, c0) in kr_dsts:
                            col = c0 + s * 128
                            evac_copy(dst[64:96, col:col + np_], pt_[64:96, 768:768 + np_], [pc], [(dname, col // 128)])

                    if dbg_stage < 2:
                        return []
                    for s in range(nsub):
                        part_a(s)
                        if s >= 1:
                            part_b(s - 1)
                    part_b(nsub - 1)
                    while nxt_a:
                        nxt_a.pop(0)()
                    if nxt is not None:
                        for s_ in range(max(1, nxt[1] // 128)):
                            nrm_b(nxt[1], s_)
                    if dbg_stage < 3:
                        return []
                    dma('sp', out_ckv.rearrange("(s p) c -> p s c", p=np_), CKVO[b][0:np_, 0:nsub, :],
                        [('ckvo', b, s) for s in range(nsub)], [])
                    dma('sp', out_kr.rearrange("(s p) c -> p s c", p=np_), KRO[b][0:np_, 0:nsub, :],
                        [('kro', b, s) for s in range(nsub)], [], odd=True)
                    cq_cells = [('cqnT', b, s) for s in range(nsub)]
                    if dbg_stage < 4:
                        return []

                    def q_nope(pr):
                        ps1, c1 = PF.get()
                        for kc in range(4):
                            mm(ps1[:, 0:TT], w_uqn_sb[:, kc, pr * 128:(pr + 1) * 128], cqnT[:, kc, 0:TT],
                               kc == 0, kc == 3, cq_cells + ['w_uq'], [c1])
                        qn = QN[pr % 2]; qnc = ('qn', pr % 2)
                        cp('act', qn[:, 0:TT], ps1[:, 0:TT], [c1], [qnc])
                        dma('sp', qdst('n', pr), qn[:, 0:TT], [qnc], [('qscn', pr)])

                    def q_rope(qd):
                        ps1, c1 = PF.get()
                        ps2, c2 = PF.get()
                        for kc in range(4):
                            mm(ps1[:, 0:TT], w_uqr_sb[:, kc, qd * 128:(qd + 1) * 128], cqnT[:, kc, 0:TT],
                               kc == 0, kc == 3, cq_cells + ['w_uq'], [c1])
                        for kc in range(4):
                            mm(ps2[:, 0:TT], w_uqs_sb[:, kc, qd * 128:(qd + 1) * 128], cqnT[:, kc, 0:TT],
                               kc == 0, kc == 3, cq_cells + ['w_uqs'], [c2])
                        t1 = T1[qd % 2]; t2 = T2[qd % 2]
                        tt('dve', t1[:, 0:TT], ps1[:, 0:TT], RFC[b][:, 0:TT], ALU.mult, [c1, ('rfc', b)], [('t1', qd % 2)])
                        tt('dve', t2[:, 0:TT], ps2[:, 0:TT], RFS[b][:, 0:TT], ALU.mult, [c2, ('rfs', b)], [('t2', qd % 2)])
                        qr = QR[qd % 2]; qrc = ('qr', qd % 2)
                        tt('dve', qr[:, 0:TT], t1[:, 0:TT], t2[:, 0:TT], ALU.add, [('t1', qd % 2), ('t2', qd % 2)], [qrc])
                        dma('sp', qdst('r', qd), qr[:, 0:TT], [qrc], [('qscr', qd)])

                    def part(k):
                        def f():
                            q_nope(2 * k)
                            q_nope(2 * k + 1)
                            q_rope(k)
                        return f
                    return [part(k) for k in range(4)]

                tiles = [dict(x=x_s, TT=TS, rt0=T)]
                for tt_i in range(8):
                    tiles.append(dict(x=x_p[tt_i * 512:(tt_i + 1) * 512, :], TT=512, rt0=tt_i * 512))
                tiles = tiles[:dbg_tiles]
                load_tile(0, tiles[0]['x'], tiles[0]['TT'], tiles[0]['rt0'])
                load_rf(0, tiles[0]['TT'], tiles[0]['rt0'])
                norm_full(0, tiles[0]['TT'])
                pending = []
                for it, td in enumerate(tiles):
                    if it + 1 < len(tiles):
                        nx = tiles[it + 1]
                        load_tile(it + 1, nx['x'], nx['TT'], nx['rt0'])
                    nxt = (it + 1, tiles[it + 1]['TT']) if it + 1 < len(tiles) else None
                    if it == 0:
                        newq = phaseA_tile(0, TS, ckvT_s, 'ckvT_s', PAST, [(KsB[0], 'Ks0_r', PAST), (KsB[1], 'Ks1_r', PAST)], nckv_s, nkr_s,
                                           lambda k, i: (qscn_s if k == 'n' else qscr_s)[i], pending, nxt)
                        sample_past()
                    else:
                        ti = it - 1
                        newq = phaseA_tile(it, 512, ckvT_p, 'ckvT_p', ti * 512,
                                           [(Kb[0], 'Kb0_r', ti * 512), (Kb[1], 'Kb1_r', ti * 512)],
                                           nckv_p[ti * 512:(ti + 1) * 512, :], nkr_p[ti * 512:(ti + 1) * 512, :],
                                           (lambda ti_: (lambda k, i: (qscn if k == 'n' else qscr)[i][:, ti_ * 512:(ti_ + 1) * 512]))(ti),
                                           pending, nxt)
                    while pending:
                        pending.pop(0)()
                    pending = list(newq)
                    if it + 1 < len(tiles):
                        nx = tiles[it + 1]
                        load_rf(it + 1, nx['TT'], nx['rt0'])
                while pending:
                    pending.pop(0)()
                if 'A' in phases:
                    P.run(final=(phases == 'A'))
                else:
                    P._reset()
            with ExitStack() as sB:
                def sbb(name, shape, dt):
                    return sB.enter_context(nc.sbuf_tensor(name, list(shape), dt))
                Vb = [sbb("Vb%d" % i, [128, 32, 128], BF16) for i in range(2)]
                VsB = [sbb("Vs%d" % i, [128, 9, 128], BF16) for i in range(2)]
                Qb = [sbb("Qb%d" % i, [96, T], BF16) for i in range(2)]
                QsB = [sbb("Qs_sb%d" % i, [96, TS], BF16) for i in range(2)]
                PT = [sbb("PT%d" % i, [128, 1024], BF16) for i in range(4)]
                RC = [sbb("RC%d" % i, [64, 512], F32) for i in range(2)]
                OT = [sbb("OT%d" % i, [64, 512], BF16) for i in range(2)]
                psS = [sB.enter_context(nc.psum_tensor("psS%d" % i, [128, 1024], F32)) for i in range(3)]
                psAcc = [sB.enter_context(nc.psum_tensor("psAcc%d" % i, [128, 512], F32)) for i in range(2)]
                SP_ = RR([(psS[i], 'psS%d' % i) for i in range(3)])
                AP_ = RR([(psAcc[i], 'psAcc%d' % i) for i in range(2)])
                GP_ = SP_
                PTR = RR([(PT[i], 'PT%d' % i) for i in range(4)])
                ORR = RR([(RC[i], OT[i], 'RCOT%d' % i) for i in range(2)])
                mset('dve', Vb[0][:, :, 64:128], 1.0, ['Vb0'])
                mset('dve', Vb[1][:, :, 64:128], 1.0, ['Vb1'])
                mset('dve', VsB[0][:, :, 64:128], 1.0, ['Vs0'])
                mset('dve', VsB[1][:, :, 64:128], 1.0, ['Vs1'])
                conv_list = list(pan_spec.values()) if conv else []

                def emit_conv(k, paced_on):
                    for (pi, src, r0, c0) in conv_list[k:k + 3]:
                        dma('pool', wsc[pi].rearrange("p (kc c) -> p kc c", kc=8),
                            src[r0:r0 + 1024, c0:c0 + 512].rearrange("(kc p) c -> p kc c", p=128),
                            paced_on, [('wsc', pi)], carry=True)
                gi = [0]

                def gen_head(h, Kbuf, kcell, Vbuf, vcell, ckvT, L, Qbuf, qcell, qsrc, NQ):
                    pieces = []

                    def kpiece(kb):
                        def f():
                            n = min(512, L - kb * 512)
                            ps, pc = GP_.get()
                            for kc in range(2):
                                mm(ps[0:64, 0:n], w_ukv_sb[:, kc, h * 128:h * 128 + 64],
                                   ckvT[:, kc, kb * 512:kb * 512 + n], kc == 0, kc == 1, ['w_ukv'], [pc])
                            cp('dve', Kbuf[0:64, kb * 512:kb * 512 + n], ps[0:64, 0:n], [pc], [(kcell, kb)])
                        return f

                    def vpiece(g0):
                        def f():
                            nkt = (L + 127) // 128
                            ps, pc = GP_.get()
                            full = True
                            cnt = min(8, nkt - g0)
                            for j in range(cnt):
                                kt = g0 + j
                                nk = min(128, L - kt * 128)
                                if nk < 128:
                                    full = False
                                for kc in range(2):
                                    mm(ps[0:nk, j * 64:(j + 1) * 64], ckvT[:, kc, kt * 128:kt * 128 + nk],
                                       w_ukv_sb[:, kc, h * 128 + 64:h * 128 + 128], kc == 0, kc == 1, ['w_ukv'], [pc])
                            npar = 128 if full else min(128, L - g0 * 128)
                            cp('dve', Vbuf[0:npar, g0:g0 + cnt, 0:64],
                               ps[0:npar, 0:cnt * 64].rearrange("p (j d) -> p j d", d=64), [pc], [(vcell, g0 // 8)])
                        return f

                    qn_, qr_ = qsrc
                    dma('sp', Qbuf[0:64, 0:NQ], qn_[h // 2][(h % 2) * 64:(h % 2) * 64 + 64, :], [], [qcell])
                    dma('sp', Qbuf[64:96, 0:NQ], qr_[h // 4][(h % 4) * 32:(h % 4) * 32 + 32, :], [], [(qcell, 'r')])
                    nkb = (L + 511) // 512
                    nkt = (L + 127) // 128
                    kps = [kpiece(kb) for kb in range(nkb)]
                    vps = [vpiece(g0) for g0 in range(0, nkt, 8)]
                    while kps or vps:
                        for _ in range(2):
                            if kps:
                                pieces.append(kps.pop(0))
                        if vps:
                            pieces.append(vps.pop(0))
                    return pieces

                LA = 2

                def attn_head(h, Kbuf, kcell, krcells, Vbuf, vcell, Qbuf, qcell, blocks, odst, pieces=()):
                    items = []
                    for bi, (q0, nq, tl) in enumerate(blocks):
                        k = 0
                        while k < len(tl):
                            if k + 1 < len(tl) and tl[k][1] == 128 and tl[k + 1][1] == 128:
                                grp = [tl[k], tl[k + 1]]
                                k += 2
                            else:
                                grp = [tl[k]]
                                k += 1
                            items.append((bi, q0, nq, len(tl), grp))
                    accs = {}
                    pend = {}
                    done = {}

                    def stage1(i):
                        (bi, q0, nq, n, grp) = items[i]
                        st, sc = SP_.get()
                        pt_, pc = PTR.get()
                        c0p = grp[0][2]
                        for hf, (kt, nk, c0, diag) in enumerate(grp):
                            mm(st[0:nk, hf * 512 + c0p:hf * 512 + nq], Kbuf[0:96, kt * 128:kt * 128 + nk],
                               Qbuf[0:96, q0 + c0p:q0 + nq], True, True,
                               [(kcell, kt // 4), qcell, (qcell, 'r')] + krcells, [sc])
                        if len(grp) == 2:
                            act(pt_[:, :].rearrange("p (h c) -> p h c", h=2)[:, :, c0p:nq],
                                st[:, :].rearrange("p (h c) -> p h c", h=2)[:, :, c0p:nq], AF.Exp, [sc], [pc],
                                scale=ATTN_SCALE)
                        else:
                            nk = grp[0][1]
                            act(pt_[0:nk, c0p:nq], st[0:nk, c0p:nq], AF.Exp, [sc], [pc], scale=ATTN_SCALE)
                        for hf, (kt, nk, c0, diag) in enumerate(grp):
                            if diag:
                                mset('dve', pt_[64:128, hf * 512 + c0:hf * 512 + c0 + 64], 0.0, [pc])
                        pend[i] = (pt_, pc)

                    def stage2(i):
                        (bi, q0, nq, n, grp) = items[i]
                        if bi not in accs:
                            accs[bi] = AP_.get()
                            done[bi] = 0
                        acc, ac = accs[bi]
                        pt_, pc = pend.pop(i)
                        for hf, (kt, nk, c0, diag) in enumerate(grp):
                            first = done[bi] == 0
                            done[bi] += 1
                            last = done[bi] == n
                            mm(acc[:, c0:nq], Vbuf[0:nk, kt, :], pt_[0:nk, hf * 512 + c0:hf * 512 + nq], first, last,
                               [(vcell, kt // 8), vcell, pc], [ac])
                        if done[bi] == n:
                            rc, ot, oc = ORR.get()
                            recip(rc[0:64, 0:nq], acc[64:128, 0:nq], [ac], [oc])
                            tt('dve', ot[0:64, 0:nq], acc[0:64, 0:nq], rc[0:64, 0:nq], ALU.mult, [ac, oc], [oc])
                            dma('sp', odst(q0, nq), ot[0:64, 0:nq], [oc], [('osc', h)])

                    pieces = list(pieces)
                    every = max(1, (len(items) - 4) // (len(pieces) + 1)) if pieces else 0
                    for i in range(len(items) + LA):
                        if i < len(items):
                            stage1(i)
                        if i >= LA:
                            stage2(i - LA)
                        if pieces and i > 0 and i % every == 0:
                            pieces.pop(0)()
                    while pieces:
                        pieces.pop(0)()

                pblocks = []
                for qb in range(8):
                    tl = [(kt, 128, 0, False) for kt in range(4 * qb)]
                    tl += [(4 * qb + j, 128, 128 * j, True) for j in range(4)]
                    pblocks.append((qb * 512, 512, tl))
                sblocks = [(0, TS, [(kt, 128, 0, False) for kt in range(8)] + [(8, 64, 0, False)])]
                krc_p = [[('Kb%d_r' % i, c) for c in range(32)] for i in range(2)]
                krc_s = [('Ks_r', c) for c in range(9)]

                def odst_p(h):
                    return lambda q0, nq: osc[h // 2][(h % 2) * 64:(h % 2) * 64 + 64, q0:q0 + nq]

                def odst_s(h):
                    return lambda q0, nq: osc_s[h // 2][(h % 2) * 64:(h % 2) * 64 + 64, q0:q0 + nq]

                def gen_s(h):
                    i = h % 2
                    return gen_head(h, KsB[i], 'Ks%d_n' % i, VsB[i], 'Vs%d' % i, ckvT_s, PAST + TS, QsB[i], 'Qs%d' % i,
                                    (qscn_s, qscr_s), TS)
                for pc_ in gen_s(0):
                    pc_()
                for h in range(NH):
                    i = h % 2
                    if h + 1 < NH:
                        nxt = gen_s(h + 1)
                    else:
                        nxt = gen_head(0, Kb[0], 'Kb0_n', Vb[0], 'Vb0', ckvT_p, T, Qb[0], 'Qb0', (qscn, qscr), T)
                    attn_head(h, KsB[i], 'Ks%d_n' % i, [('Ks%d_r' % i, c) for c in range(9)], VsB[i], 'Vs%d' % i,
                              QsB[i], 'Qs%d' % i, sblocks, odst_s(h), nxt)
                for h in range(NH):
                    i = h % 2
                    nxt = ()
                    if h + 1 < NH:
                        j = (h + 1) % 2
                        nxt = gen_head(h + 1, Kb[j], 'Kb%d_n' % j, Vb[j], 'Vb%d' % j, ckvT_p, T, Qb[j], 'Qb%d' % j,
                                       (qscn, qscr), T)
                    attn_head(h, Kb[i], 'Kb%d_n' % i, krc_p[i], Vb[i], 'Vb%d' % i, Qb[i], 'Qb%d' % i, pblocks,
                              odst_p(h), nxt)
                    emit_conv(3 * h, [('osc', h)])
                if 'B' in phases:
                    P.run(final=('C' not in phases))
                else:
                    P._reset()

        with ExitStack() as sC:
            def sc_(name, shape, dt):
                return sC.enter_context(nc.sbuf_tensor(name, list(shape), dt))
            NB = 6
            WR = [sc_("wr%d" % i, [128, 8, 512], BF16) for i in range(NB)]
            ident = sc_("identC", [128, 128], BF16)
            gcols = sc_("gcols", [128, 4, 8], F32)
            gfin_b = sc_("gfin_b", [128, D], F32)
            wconv_c = sc_("wconv_c", [128, 3, 8], F32)
            XTB = [sc_("XTm%d" % i, [128, 4, D], F32) for i in range(2)]
            XN = [sc_("xnC%d" % i, [128, D], BF16) for i in range(4)]
            junk = sc_("junkC", [128, D], BF16)
            actT = sc_("actT", [128, 8, 512], BF16)
            oT = sc_("oT", [128, 8, 512], BF16)
            wT = sc_("wT", [128, 8, 512], BF16)
            mT = sc_("mT", [128, 8, 512], BF16)
            VJ = [sc_("vj%d" % i, [128, 514], F32) for i in range(2)]
            TMPF = [sc_("tmpf%d" % i, [128, 512], F32) for i in range(4)]
            halo = sc_("halo", [128, 8, 2], F32)
            qmT = sc_("qmT", [128, 8, 512], BF16)
            PTm = [sc_("ptm%d" % i, [128, 2, 512], BF16) for i in range(2)]
            omT = sc_("omT", [128, 8, 512], BF16)
            memKT = sc_("memKT", [128, 8, NMEM], BF16)
            memV = sc_("memV", [128, 2, D], BF16)
            memtmp = sc_("memtmp", [128, 2, D], F32)
            hid = sc_("hid", [128, 32, 512], BF16)
            membf = hid[:, 0:4, :].rearrange("p (s a) b -> p s (a b)", s=2)
            ones = sc_("ones", [128, 128], BF16)
            STAT = sc_("statC", [128, 48], F32)
            psf = [sC.enter_context(nc.psum_tensor("psfC%d" % i, [128, 512], F32)) for i in range(6)]
            psb = [sC.enter_context(nc.psum_tensor("psbC%d" % i, [128, 1024], BF16)) for i in range(2)]
            PF = RR([(psf[i], 'psf%d' % i) for i in range(6)])
            PB = RR([(psb[i], 'psb%d' % i) for i in range(2)])

            dma('pool', ident[:], identf, [], ['ident'])
            dma('sp', gcols[:], gcols_h.rearrange("p (g c) -> p g c", g=4), [], ['gcols'], odd=True)
            dma('sp', gfin_b[:], g_final.partition_broadcast(128), [], ['gfin_b'], odd=True)
            dma('sp', wconv_c[:], wconv_h.rearrange("p (k c) -> p k c", k=3), [], ['wconv_c'], odd=True)
            mset('dve', ones[:], 1.0, ['ones'])

            wri = [0]

            def load_panel(name):
                pi = pan_spec[name][0]
                k = wri[0] % NB
                wri[0] += 1
                dma('sp', WR[k][:], wsc[pi].rearrange("p (kc c) -> p kc c", kc=8), [('wsc', pi)], [('wr', k)])
                return WR[k], ('wr', k)

            def norm_a(src_ap, src_cells, s, np_, scol):
                stc = ('stat', scol)
                act(junk[0:np_, :], src_ap, AF.Square, src_cells + [stc], ['junkC', stc],
                    accum=STAT[0:np_, scol + s:scol + s + 1])
                rstd_ops(STAT[0:np_, scol + s:scol + s + 1], STAT[0:np_, scol + 4 + s:scol + 5 + s],
                         STAT[0:np_, scol + 8 + s:scol + 9 + s], float(D), stc)
                ts('dve', XN[s % 4][0:np_, :], src_ap, STAT[0:np_, scol + 8 + s:scol + 9 + s], ALU.mult,
                   src_cells + [stc], [('xn', s % 4)])

            def norm_b(s, np_, gidx, dstT, dst_cell):
                xn = XN[s % 4]; xnc = ('xn', s % 4)
                pt_, pc = PB.get()
                for c in range(8):
                    tr(pt_[:, c * 128:c * 128 + np_], xn[0:np_, c * 128:(c + 1) * 128], ident[0:np_, 0:np_],
                       [xnc, 'ident'], [pc])
                tt('dve', dstT[:, :, s * 128:s * 128 + np_],
                   pt_[:, :].rearrange("p (c t) -> p c t", c=8)[:, :, 0:np_],
                   gcols[:, gidx, :].unsqueeze(2).broadcast_to([128, 8, np_]), ALU.mult,
                   [pc, 'gcols'], [(dst_cell, s)])

            def norm_T(src_fn, np_, nsub, gidx, dstT, dst_cell, src_cells, scol):
                mset('dve', STAT[:, scol:scol + nsub], 0.0, [('stat', scol)])
                for s in range(nsub):
                    norm_a(src_fn(s), src_cells, s, np_, scol)
                    if s >= 1:
                        norm_b(s - 1, np_, gidx, dstT, dst_cell)
                norm_b(nsub - 1, np_, gidx, dstT, dst_cell)

            def prep_mem_prompt():
                dma('sp', memtmp[:], memp.rearrange("(s p) d -> p s d", p=128), [], ['memtmp'])
                norm_T(lambda s: memtmp[:, s, :], 128, 2, 3, actT, 'actT', ['memtmp'], 12)
                for (pfx, dst_out, is_k) in (('km', nmk_p, True), ('vm', nmv_p, False)):
                    for half in range(2):
                        wp, wc = load_panel('%s%d' % (pfx, half))
                        for s in range(2):
                            ps, pc = PF.get()
                            for kc in range(8):
                                mm(ps[:, :], actT[:, kc, s * 128:(s + 1) * 128], wp[:, kc, :], kc == 0, kc == 7,
                                   [('actT', s), wc], [pc])
                            cp('act', memtmp[:, s, half * 512:(half + 1) * 512], ps[:, :], [pc], ['memtmp'])
                    dma('pool', dst_out.rearrange("(s p) d -> p s d", p=128), memtmp[:], ['memtmp'], [])
                    if is_k:
                        cp('dve', membf, memtmp[:], ['memtmp'], ['membf'] + [('hid', c_) for c_ in range(4)])
                        for s in range(2):
                            pt_, pc = PB.get()
                            for c in range(8):
                                tr(pt_[:, c * 128:(c + 1) * 128], membf[:, s, c * 128:(c + 1) * 128], ident[:, :],
                                   ['membf', 'ident'], [pc])
                            cp('dve', memKT[:, :, s * 128:(s + 1) * 128],
                               pt_[:, :].rearrange("p (c t) -> p c t", c=8), [pc], ['memKT'])
                    else:
                        cp('dve', memV[:], memtmp[:], ['memtmp'], ['memV'])

            def prep_mem_sample():
                dma('pool', membf, cmk.rearrange("(s p) d -> p s d", p=128), [], ['membf'] + [('hid', c_) for c_ in range(4)])
                dma('pool', memV[:], cmv.rearrange("(s p) d -> p s d", p=128), [], ['memV'])
                for s in range(2):
                    pt_, pc = PB.get()
                    for c in range(8):
                        tr(pt_[:, c * 128:(c + 1) * 128], membf[:, s, c * 128:(c + 1) * 128], ident[:, :],
                           ['membf', 'ident'], [pc])
                    cp('dve', memKT[:, :, s * 128:(s + 1) * 128],
                       pt_[:, :].rearrange("p (c t) -> p c t", c=8), [pc], ['memKT'])

            def load_x(it, x_rows, TT):
                nsub = max(1, TT // 128); np_ = min(TT, 128)
                dma('sp', XTB[it % 2][0:np_, 0:nsub, :], x_rows.rearrange("(s p) d -> p s d", p=np_), [],
                    [(('XTm', it % 2), s_) for s_ in range(4)])

            def norm1_a(it, TT):
                nsub = max(1, TT // 128); np_ = min(TT, 128)
                XTm = XTB[it % 2]; xc = ('XTm', it % 2)
                mset('dve', STAT[:, 0:nsub], 0.0, [('stat', 0)])
                for s in range(nsub):
                    norm_a(XTm[0:np_, s, :], [(xc, s)], s, np_, 0)

            def norm1_b(it, TT):
                nsub = max(1, TT // 128); np_ = min(TT, 128)
                for s in range(nsub):
                    norm_b(s, np_, 0, actT, 'actT')

            def phaseC_tile(it, y_rows, TT, o_src, last_conv_out, hooks=None):
                nsub = max(1, TT // 128); np_ = min(TT, 128)
                N = TT
                XTm = XTB[it % 2]; xc = ('XTm', it % 2)
                AT = [('actT', s_) for s_ in range(nsub)]
                dma('sp', oT[:, :, 0:N], o_src, [], ['oT'])
                for half in range(2):
                    pu, cu = load_panel('u%d' % half)
                    pgc, cgc = load_panel('gc%d' % half)
                    pgb, cgb = load_panel('gb%d' % half)
                    for jj in range(4):
                        j = half * 4 + jj
                        psu, c_u = PF.get(); psc, c_c = PF.get(); psg, c_g = PF.get()
                        for (ps, pc, wp, wc) in ((psu, c_u, pu, cu), (psc, c_c, pgc, cgc), (psg, c_g, pgb, cgb)):
                            for kc in range(8):
                                mm(ps[:, 0:N], wp[:, kc, jj * 128:(jj + 1) * 128], actT[:, kc, 0:N], kc == 0, kc == 7,
                                   AT + [wc], [pc])
                        ut = TMPF[j % 2]; utc = ('tmpf', j % 2)
                        cp('act', ut[:, 0:N], psu[:, 0:N], [c_u], [utc])
                        vj = VJ[j % 2]; vjc = ('vj', j % 2)
                        vjh = ('vjh', j % 2)
                        cp('pool', vj[:, 0:2], halo[:, j, :], ['halo'], [vjh])
                        tt('dve', vj[:, 2:2 + N], ut[:, 0:N], psc[:, 0:N], ALU.mult, [utc, c_c], [vjc])
                        ct = TMPF[2 + j % 2]; ctc = ('tmpf', 2 + j % 2)
                        ts('dve', ct[:, 0:N], vj[:, 2:2 + N], wconv_c[:, 2, j:j + 1], ALU.mult, [vjc, 'wconv_c'], [ctc])
                        stt('dve', ct[:, 0:N], vj[:, 1:1 + N], wconv_c[:, 1, j:j + 1], ct[:, 0:N], ALU.mult, ALU.add,
                            [vjc, vjh, 'wconv_c', ctc], [ctc])
                        stt('dve', ct[:, 0:N], vj[:, 0:N], wconv_c[:, 0, j:j + 1], ct[:, 0:N], ALU.mult, ALU.add,
                            [vjc, vjh, 'wconv_c', ctc], [ctc])
                        cp('pool', halo[:, j, :], vj[:, N:N + 2], [vjc], ['halo'])
                        tt('dve', wT[:, j, 0:N], ct[:, 0:N], psg[:, 0:N], ALU.mult, [ctc, c_g], ['wT'])
                if last_conv_out is not None:
                    dma('pool', last_conv_out.rearrange("p (c t) -> p c t", c=8), halo[:], ['halo'], [], odd=True)
                for half in range(2):
                    pco, cco = load_panel('co%d' % half)
                    pac, cac = load_panel('ac%d' % half)
                    pmo, cmo = load_panel('mo%d' % half)
                    pam, cam = load_panel('am%d' % half)
                    for jj in range(4):
                        i = half * 4 + jj
                        psya, c_ya = PF.get(); psac, c_ac = PF.get(); psyb, c_yb = PF.get(); psam, c_am = PF.get()
                        for (ps, pc, wp, wc, src, srcc) in ((psya, c_ya, pco, cco, wT, ['wT']),
                                                            (psac, c_ac, pac, cac, actT, AT),
                                                            (psyb, c_yb, pmo, cmo, oT, ['oT']),
                                                            (psam, c_am, pam, cam, actT, AT)):
                            for kc in range(8):
                                mm(ps[:, 0:N], wp[:, kc, jj * 128:(jj + 1) * 128], src[:, kc, 0:N], kc == 0, kc == 7,
                                   srcc + [wc], [pc])
                        th1 = TMPF[0]; th2 = TMPF[1]
                        act(th1[:, 0:N], psac[:, 0:N], AF.Tanh, [c_ac], [('tmpf', 0)], scale=0.5)
                        act(th2[:, 0:N], psam[:, 0:N], AF.Tanh, [c_am], [('tmpf', 1)], scale=0.5)
                        stt('dve', th1[:, 0:N], th1[:, 0:N], 1.0, psya[:, 0:N], ALU.add, ALU.mult,
                            [('tmpf', 0), c_ya], [('tmpf', 0)])
                        stt('dve', th2[:, 0:N], th2[:, 0:N], 1.0, psyb[:, 0:N], ALU.add, ALU.mult,
                            [('tmpf', 1), c_yb], [('tmpf', 1)])
                        tt('pool', mT[:, i, 0:N], th1[:, 0:N], th2[:, 0:N], ALU.add, [('tmpf', 0), ('tmpf', 1)], ['mT'])
                def update_and_norm(srcT, src_cells, pname, half_scale, gidx, scol):
                    pans = [load_panel('%s%d' % (pname, half)) for half in range(2)]
                    mset('dve', STAT[:, scol:scol + nsub], 0.0, [('stat', scol)])
                    for s in range(nsub):
                        for half, (pw, cw) in enumerate(pans):
                            ps, pc = PF.get()
                            for kc in range(8):
                                mm(ps[0:np_, :], srcT[:, kc, s * 128:s * 128 + np_], pw[:, kc, :], kc == 0, kc == 7,
                                   src_cells + [cw], [pc])
                            xs_ = XTm[0:np_, s, half * 512:(half + 1) * 512]
                            if half_scale is None:
                                tt('dve', xs_, ps[0:np_, :], xs_, ALU.add, [pc, (xc, s)], [(xc, s)])
                            else:
                                stt('dve', xs_, ps[0:np_, :], half_scale, xs_, ALU.mult, ALU.add,
                                    [pc, (xc, s)], [(xc, s)])
                        norm_a(XTm[0:np_, s, :], [(xc, s)], s, np_, scol)
                        if s >= 1:
                            norm_b(s - 1, np_, gidx, actT, 'actT')
                    norm_b(nsub - 1, np_, gidx, actT, 'actT')

                if hooks is not None:
                    hooks['load_next']()
                update_and_norm(mT, ['mT'], 'mx', 0.5, 1, 12)
                def fm_group(pan, pcell, split, evac):
                    banks = [PF.get() for _ in range(4)]
                    if split and nsub == 4:
                        c_sp = 3 * 128
                        for jj in range(4):
                            ps, pc = banks[jj]
                            for kc in range(8):
                                mm(ps[:, 0:c_sp], pan[:, kc, jj * 128:(jj + 1) * 128], actT[:, kc, 0:c_sp],
                                   kc == 0, kc == 7, AT[0:3] + [pcell], [pc])
                        for jj in range(4):
                            ps, pc = banks[jj]
                            for kc in range(8):
                                mm(ps[:, c_sp:N], pan[:, kc, jj * 128:(jj + 1) * 128], actT[:, kc, c_sp:N],
                                   kc == 0, kc == 7, AT[3:4] + [pcell], [pc])
                            evac(jj, ps, pc)
                    else:
                        for jj in range(4):
                            ps, pc = banks[jj]
                            for kc in range(8):
                                mm(ps[:, 0:N], pan[:, kc, jj * 128:(jj + 1) * 128], actT[:, kc, 0:N], kc == 0, kc == 7,
                                   AT + [pcell], [pc])
                            evac(jj, ps, pc)

                for half in range(2):
                    pq, cq_ = load_panel('qm%d' % half)

                    def ev_q(jj, ps, pc, half=half):
                        i = half * 4 + jj
                        cp('act', qmT[:, i, 0:N], ps[:, 0:N], [pc], [('qmT', i)])
                    fm_group(pq, cq_, half == 0, ev_q)
                def memS(hm):
                    ptm = PTm[hm % 2]; ptc = ('ptm', hm % 2)
                    for kt in range(2):
                        ps, pc = PF.get()
                        for dc in range(2):
                            mm(ps[:, 0:N], memKT[:, hm * 2 + dc, kt * 128:(kt + 1) * 128], qmT[:, hm * 2 + dc, 0:N],
                               dc == 0, dc == 1, ['memKT', ('qmT', hm * 2 + dc)], [pc])
                        act(ptm[:, kt, 0:N], ps[:, 0:N], AF.Exp, [pc], [ptc], scale=MEM_SCALE)

                memS(0)
                for hm in range(4):
                    if hm + 1 < 4:
                        memS(hm + 1)
                    ptm = PTm[hm % 2]; ptc = ('ptm', hm % 2)
                    pss, pcs = PF.get()
                    for kt in range(2):
                        mm(pss[:, 0:N], ones[:, :], ptm[:, kt, 0:N], kt == 0, kt == 1, ['ones', ptc], [pcs])
                    rcb = TMPF[hm % 2]; rcc = ('tmpf', hm % 2)
                    recip(rcb[:, 0:N], pss[:, 0:N], [pcs], [rcc])
                    for dc in range(2):
                        ps, pc = PF.get()
                        for kt in range(2):
                            mm(ps[:, 0:N], memV[:, kt, hm * 256 + dc * 128:hm * 256 + (dc + 1) * 128], ptm[:, kt, 0:N],
                               kt == 0, kt == 1, ['memV', ptc], [pc])
                        tt('dve', omT[:, hm * 2 + dc, 0:N], ps[:, 0:N], rcb[:, 0:N], ALU.mult, [pc, rcc], ['omT'])
                update_and_norm(omT, ['omT'], 'om', None, 2, 24)
                if hooks is not None:
                    hooks['norm_a']()
                for jp in range(8):
                    pup, cup = load_panel('up%d' % jp)

                    def ev_u(jj, ps, pc, jp=jp):
                        c = jp * 4 + jj
                        rt = TMPF[2 + c % 2]; rtc = ('tmpf', 2 + c % 2)
                        act(rt[:, 0:N], ps[:, 0:N], AF.Relu, [pc], [rtc])
                        tt('pool', hid[:, c, 0:N], rt[:, 0:N], rt[:, 0:N], ALU.mult, [rtc], [('hid', c)])
                    fm_group(pup, cup, jp == 0, ev_u)
                if hooks is not None:
                    hooks['norm_b']()
                for half in range(2):
                    accs = [PF.get() for _ in range(nsub)]
                    for g in range(4):
                        pdn, cdn = load_panel('dn%d_%d' % (g, half))
                        for s in range(nsub):
                            ps, pc = accs[s]
                            for kc in range(8):
                                kk = g * 8 + kc
                                mm(ps[0:np_, :], hid[:, kk, s * 128:s * 128 + np_], pdn[:, kc, :], kk == 0, kk == 31,
                                   [('hid', kk), cdn], [pc])
                    for s in range(nsub):
                        ps, pc = accs[s]
                        tt('dve', XTm[0:np_, s, half * 512:(half + 1) * 512], ps[0:np_, :],
                           XTm[0:np_, s, half * 512:(half + 1) * 512], ALU.add, [pc, (xc, s)], [(xc, s)])
                stc = ('stat', 36)
                mset('dve', STAT[:, 36:40], 0.0, [stc])
                for s in range(nsub):
                    act(junk[0:np_, :], XTm[0:np_, s, :], AF.Square, [(xc, s), stc], ['junkC', stc],
                        accum=STAT[0:np_, 36 + s:37 + s])
                    rstd_ops(STAT[0:np_, 36 + s:37 + s], STAT[0:np_, 40 + s:41 + s], STAT[0:np_, 44 + s:45 + s],
                             float(D), stc)
                    stt('dve', XTm[0:np_, s, :], XTm[0:np_, s, :], STAT[0:np_, 44 + s:45 + s], gfin_b[0:np_, :],
                        ALU.mult, ALU.mult, [(xc, s), stc, 'gfin_b'], [(xc, s)])
                dma('pool', y_rows.rearrange("(s p) d -> p s d", p=np_), XTm[0:np_, 0:nsub, :],
                    [(xc, s_) for s_ in range(nsub)], [])

            xs_list = [(x_s, TS)] + [(x_p[ti * 512:(ti + 1) * 512, :], 512) for ti in range(8)]
            load_x(0, *xs_list[0])
            prep_mem_sample()
            dma('sp', halo[:], cconv_h.rearrange("p (c t) -> p c t", c=8), [], ['halo'], odd=True)
            load_x(1, *xs_list[1])
            norm1_a(0, TS)
            norm1_b(0, TS)
            phaseC_tile(0, y_s, TS, osc_s.rearrange("c p t -> p c t"), nconv_s)
            prep_mem_prompt()
            mset('dve', halo[:], 0.0, ['halo'])
            norm1_a(1, 512)
            norm1_b(1, 512)
            nop = lambda: None
            for ti in range(8):
                j = ti + 2
                if j < len(xs_list):
                    hk = dict(load_next=(lambda j_: (lambda: load_x(j_, *xs_list[j_])))(j),
                              norm_a=(lambda j_: (lambda: norm1_a(j_, 512)))(j),
                              norm_b=(lambda j_: (lambda: norm1_b(j_, 512)))(j))
                else:
                    hk = dict(load_next=nop, norm_a=nop, norm_b=nop)
                if ti == 0:
                    pass
                phaseC_tile(ti + 1, y_p[ti * 512:(ti + 1) * 512, :], 512,
                            osc[:, :, ti * 512:(ti + 1) * 512].rearrange("c p t -> p c t"),
                            nconv_p if ti == 7 else None, hk)
            if 'C' in phases:
                P.run(wait_carry_on='sp', final=True)
            else:
                P._reset()
    return nc


_CACHE = {}


def _rope_tables():
    half = 16
    inv = (np.float32(10000.0) ** (-np.arange(half, dtype=np.float32) / np.float32(half))).astype(np.float32)
    pos = np.concatenate([np.arange(T, dtype=np.float32), PAST + np.arange(TS, dtype=np.float32)])
    ang = (pos[:, None] * inv[None, :]).astype(np.float32)
    cos = np.cos(ang).astype(np.float32)
    sin = np.sin(ang).astype(np.float32)
    rt_tm = np.concatenate([cos, cos, sin, sin], axis=1).astype(np.float32)
    rt_c = np.ascontiguousarray(np.tile(np.concatenate([cos, cos], axis=1).T, (4, 1)))
    rt_s = np.ascontiguousarray(np.tile(np.concatenate([-sin, sin], axis=1).T, (4, 1)))
    return rt_tm, rt_c, rt_s


def kernel(x_prompt, x_sample, cache_conv, cache_ckv, cache_krope, cache_mem_k, cache_mem_v, mem_prompt,
           g_mix, w_in, w_conv, w_conv_out, g_q, w_uq, g_kv, w_ukv, w_mla_out, w_mix_out,
           g_mem_q, g_mem_kv, w_qm, w_km, w_vm, w_om, g_mlp, w_up, w_down, g_final):
    if 'nc' not in _CACHE:
        _CACHE['nc'] = build_program()
    nc = _CACHE['nc']
    in_maps = _prep(x_prompt, x_sample, cache_conv, cache_ckv, cache_krope, cache_mem_k, cache_mem_v, mem_prompt,
                    g_mix, w_in, w_conv, w_conv_out, g_q, w_uq, g_kv, w_ukv, w_mla_out, w_mix_out,
                    g_mem_q, g_mem_kv, w_qm, w_km, w_vm, w_om, g_mlp, w_up, w_down, g_final)
    res = run_bass_kernel_spmd(nc, in_maps, core_ids=list(range(N_CORES)))
    return _gather(res.results)


def _prep(x_prompt, x_sample, cache_conv, cache_ckv, cache_krope, cache_mem_k, cache_mem_v, mem_prompt,
          g_mix, w_in, w_conv, w_conv_out, g_q, w_uq, g_kv, w_ukv, w_mla_out, w_mix_out,
          g_mem_q, g_mem_kv, w_qm, w_km, w_vm, w_om, g_mlp, w_up, w_down, g_final):
    f = lambda a: np.ascontiguousarray(np.asarray(a, dtype=np.float32))
    rt_tm, rt_c, rt_s = _rope_tables()
    w_uq0 = f(w_uq[0])
    idx = np.concatenate([np.concatenate([np.arange(h * 96 + 80, h * 96 + 96), np.arange(h * 96 + 64, h * 96 + 80)])
                          for h in range(NH)])
    w_uqs = np.ascontiguousarray(w_uq0[:, idx])
    idx_n = np.concatenate([np.arange(h * 96, h * 96 + 64) for h in range(NH)])
    idx_r = np.concatenate([np.arange(h * 96 + 64, h * 96 + 96) for h in range(NH)])
    w_uqn = np.ascontiguousarray(w_uq0[:, idx_n])
    w_uqr = np.ascontiguousarray(w_uq0[:, idx_r])
    shared = dict(
        g_mix=f(g_mix[0]), w_in=f(w_in[0]), w_conv=f(w_conv[0]), w_conv_out=f(w_conv_out[0]), g_q=f(g_q[0]),
        w_uqn=w_uqn, w_uqr=w_uqr, w_uqs=w_uqs, g_kv=f(g_kv[0]), w_ukv=f(w_ukv[0]), w_mla_out=f(w_mla_out[0]),
        w_mix_out=f(w_mix_out[0]), g_mem_q=f(g_mem_q[0]), g_mem_kv=f(g_mem_kv[0]), w_qm=f(w_qm[0]),
        w_km=f(w_km[0]), w_vm=f(w_vm[0]), w_om=f(w_om[0]), g_mlp=f(g_mlp[0]), w_up=f(w_up[0]),
        w_down=f(w_down[0]), g_final=f(g_final), identf=np.eye(128, dtype=np.float32),
        rt_tm=rt_tm, rt_c=rt_c, rt_s=rt_s,
        gcols_h=np.ascontiguousarray(np.stack([f(g_mix[0]), f(g_mem_q[0]), f(g_mlp[0]), f(g_mem_kv[0])])
                                     .reshape(4, 8, 128).transpose(2, 0, 1).reshape(128, 32)),
        wconv_h=np.ascontiguousarray(f(w_conv[0]).reshape(3, 8, 128).transpose(2, 0, 1).reshape(128, 24)))
    in_maps = []
    for b in range(N_CORES):
        m = dict(shared)
        m.update(x_p=f(x_prompt[b]), x_s=f(x_sample[b]), cconv=f(cache_conv[0, b]), cckv=f(cache_ckv[0, b]),
                 ckr=f(cache_krope[0, b]),
                 cconv_h=np.ascontiguousarray(f(cache_conv[0, b]).reshape(2, 8, 128).transpose(2, 1, 0).reshape(128, 16)), cmk=f(np.asarray(cache_mem_k[0, b]).reshape(NMEM, D)),
                 cmv=f(np.asarray(cache_mem_v[0, b]).reshape(NMEM, D)), memp=f(mem_prompt[b]))
        in_maps.append(m)
    return in_maps


def _gather(R):
    st = lambda k: np.stack([np.asarray(R[b][k], dtype=np.float32) for b in range(len(R))])
    cv = lambda k: np.ascontiguousarray(st(k).reshape(len(R), 128, 8, 2).transpose(0, 3, 2, 1).reshape(len(R), 2, D))
    y_prompt = st('y_p')
    y_sample = st('y_s')
    return (y_prompt, y_sample,
            cv('nconv_p')[None], st('nckv_p')[None], st('nkr_p')[None],
            st('nmk_p').reshape(1, N_CORES, NMEM, 4, 256), st('nmv_p').reshape(1, N_CORES, NMEM, 4, 256),
            cv('nconv_s')[None], st('nckv_s')[None], st('nkr_s')[None])
```
